# Optimizing a Trainium2 kernel written in Bass

```python
import jax
import jax.numpy as jnp
from jax import lax
import numpy as np

D_MODEL = 1024
BATCH = 8
SEQ = 4096
DEPTH = 4

HEAD_DIM = 64
SB_HEADS = 8
SB_BLOCK = 128
NSA_HEADS = 8
NSA_KV_GROUPS = 2
NSA_GROUP = NSA_HEADS // NSA_KV_GROUPS
CMP_BLOCK = 32
CMP_HIDDEN = 128
SEL_BLOCK = 64
N_SEL = 16
WINDOW = 512
NSA_BLOCK = 64
N_EXPERTS = 32
TOP_K = 4
D_FF = 1024
SWIGLU_ALPHA = 1.702
SWIGLU_LIMIT = 7.0
PLE_DIM = 256
LN_EPS = 1e-5
DN_ALPHA = (2 * DEPTH) ** 0.25
DN_BETA = (8 * DEPTH) ** -0.25
NEG_INF = -1e30
FORCED_SCORE = 1e6

SB_W = SB_HEADS * HEAD_DIM
NSA_QW = NSA_HEADS * HEAD_DIM
NSA_KVW = NSA_KV_GROUPS * HEAD_DIM
NSA_GATE_W = 3 * NSA_HEADS
IN_SPLITS = (SB_W, SB_W, SB_W, NSA_QW) + (NSA_KVW,) * 6 + (NSA_GATE_W, D_MODEL, D_MODEL)
N_IN = 3 * SB_W + NSA_QW + 6 * NSA_KVW + NSA_GATE_W + 2 * D_MODEL

kernel_name = "hybrid_stickbreak_nsa_moe_deepnorm"


def layer_norm(x, g, b):
    xf = x.astype(jnp.float32)
    mu = jnp.mean(xf, axis=-1, keepdims=True)
    var = jnp.mean(jnp.square(xf - mu), axis=-1, keepdims=True)
    y = (xf - mu) * lax.rsqrt(var + LN_EPS)
    return (y * g.astype(jnp.float32) + b.astype(jnp.float32)).astype(x.dtype)


def split_columns(t):
    parts, start = [], 0
    for width in IN_SPLITS:
        parts.append(t[..., start:start + width])
        start += width
    return parts


def to_heads(t, n):
    B, S, _ = t.shape
    return t.reshape(B, S, n, HEAD_DIM).transpose(0, 2, 1, 3).astype(jnp.float32)


def alibi_slopes():
    return jnp.exp2(-8.0 * jnp.arange(1, NSA_HEADS + 1, dtype=jnp.float32) / NSA_HEADS)


def stick_breaking_attention(q, k, v):
    B, H, S, Dh = q.shape
    scale = Dh ** -0.5
    kpos = jnp.arange(S)

    def block(jb):
        qs = jb * SB_BLOCK
        qb = lax.dynamic_slice_in_dim(q, qs, SB_BLOCK, axis=2)
        tq = qs + jnp.arange(SB_BLOCK)
        mask = kpos[None, :] < tq[:, None]
        z = jnp.einsum('bhqd,bhkd->bhqk', qb, k) * scale
        log_stay = jnp.where(mask, jax.nn.log_sigmoid(-z), 0.0)
        later = lax.cumsum(log_stay, axis=3, reverse=True) - log_stay
        w = jnp.where(mask, jnp.exp(jax.nn.log_sigmoid(z) + later), 0.0)
        return jnp.einsum('bhqk,bhkd->bhqd', w, v)

    o = lax.map(block, jnp.arange(S // SB_BLOCK))
    return jnp.moveaxis(o, 0, 2).reshape(B, H, S, Dh)


def compress_tokens(kv, w1, w2, pe):
    B, G, S, Dh = kv.shape
    blk = kv.reshape(B, G, S // CMP_BLOCK, CMP_BLOCK, Dh) + pe
    hid = jax.nn.silu(jnp.einsum('bgcld,ldh->bgch', blk, w1))
    return jnp.einsum('bgch,hd->bgcd', hid, w2)


def gather_blocks(blocks, idx):
    return jax.vmap(jax.vmap(lambda blk, ix: blk[ix]))(blocks, idx)


def native_sparse_attention(q, k_c, v_c, k_s, v_s, k_w, v_w, gate_logits, w_cmp1, w_cmp2, pe_cmp):
    B, G, R, S, Dh = q.shape
    scale = Dh ** -0.5
    n_cmp = S // CMP_BLOCK
    n_blk = S // SEL_BLOCK
    n_sel = min(N_SEL, n_blk)
    slopes = alibi_slopes().reshape(G, R)[None, :, :, None, None]
    pos = jnp.arange(S)

    kc = compress_tokens(k_c, w_cmp1[0], w_cmp2[0], pe_cmp[0])
    vc = compress_tokens(v_c, w_cmp1[1], w_cmp2[1], pe_cmp[1])
    c_end = jnp.arange(n_cmp) * CMP_BLOCK + (CMP_BLOCK - 1)
    dist_c = pos[:, None] - c_end[None, :]
    mask_c = dist_c >= 0
    s_c = jnp.einsum('bgrtd,bgcd->bgrtc', q, kc) * scale - slopes * dist_c.astype(jnp.float32)
    p_c = jnp.where(mask_c, jax.nn.softmax(jnp.where(mask_c, s_c, NEG_INF), axis=-1), 0.0)
    o_cmp = jnp.einsum('bgrtc,bgcd->bgrtd', p_c, vc)

    imp = p_c.sum(axis=2).reshape(B, G, S, n_blk, SEL_BLOCK // CMP_BLOCK).sum(-1)
    blk = jnp.arange(n_blk)[None, :]
    cur = (pos // SEL_BLOCK)[:, None]
    forced = (blk == 0) | (blk == cur) | (blk == cur - 1)
    valid = blk * SEL_BLOCK <= pos[:, None]
    imp = jnp.where(forced, FORCED_SCORE, jnp.where(valid, imp, NEG_INF))
    _, sel_idx = lax.top_k(imp, n_sel)
    ks_blocks = k_s.reshape(B, G, n_blk, SEL_BLOCK, Dh)
    vs_blocks = v_s.reshape(B, G, n_blk, SEL_BLOCK, Dh)
    kw_pad = jnp.pad(k_w, ((0, 0), (0, 0), (WINDOW, 0), (0, 0)))
    vw_pad = jnp.pad(v_w, ((0, 0), (0, 0), (WINDOW, 0), (0, 0)))
    n_keys = n_sel * SEL_BLOCK

    def block(jb):
        qs = jb * NSA_BLOCK
        qb = lax.dynamic_slice_in_dim(q, qs, NSA_BLOCK, axis=3)
        tq = qs + jnp.arange(NSA_BLOCK)
        idx = lax.dynamic_slice_in_dim(sel_idx, qs, NSA_BLOCK, axis=2)
        kg = gather_blocks(ks_blocks, idx).reshape(B, G, NSA_BLOCK, n_keys, Dh)
        vg = gather_blocks(vs_blocks, idx).reshape(B, G, NSA_BLOCK, n_keys, Dh)
        kpos = (idx[..., None] * SEL_BLOCK + jnp.arange(SEL_BLOCK)).reshape(B, G, NSA_BLOCK, n_keys)
        dist_s = (tq[:, None] - kpos)[:, :, None]
        s_s = jnp.einsum('bgrqd,bgqmd->bgrqm', qb, kg) * scale - slopes * dist_s.astype(jnp.float32)
        p_s = jax.nn.softmax(jnp.where(dist_s >= 0, s_s, NEG_INF), axis=-1)
        o_s = jnp.einsum('bgrqm,bgqmd->bgrqd', p_s, vg)
        kwb = lax.dynamic_slice_in_dim(kw_pad, qs, NSA_BLOCK + WINDOW, axis=2)
        vwb = lax.dynamic_slice_in_dim(vw_pad, qs, NSA_BLOCK + WINDOW, axis=2)
        wpos = qs - WINDOW + jnp.arange(NSA_BLOCK + WINDOW)
        dist_w = tq[:, None] - wpos[None, :]
        mask_w = (dist_w >= 0) & (dist_w < WINDOW) & (wpos[None, :] >= 0)
        s_w = jnp.einsum('bgrqd,bgkd->bgrqk', qb, kwb) * scale - slopes * dist_w.astype(jnp.float32)
        p_w = jax.nn.softmax(jnp.where(mask_w, s_w, NEG_INF), axis=-1)
        o_w = jnp.einsum('bgrqk,bgkd->bgrqd', p_w, vwb)
        return o_s, o_w

    o_sel, o_win = lax.map(block, jnp.arange(S // NSA_BLOCK))
    o_sel = jnp.moveaxis(o_sel, 0, 3).reshape(B, G, R, S, Dh)
    o_win = jnp.moveaxis(o_win, 0, 3).reshape(B, G, R, S, Dh)
    g = jax.nn.sigmoid(gate_logits)
    return g[..., 0:1] * o_cmp + g[..., 1:2] * o_sel + g[..., 2:3] * o_win


def moe_ffn(h, w_router, b_router, w_gu, b_gu, w_down, b_down):
    B, S, D = h.shape
    t = h.reshape(B * S, D)
    logits = (t @ w_router + b_router).astype(jnp.float32)
    top_val, top_idx = lax.top_k(logits, TOP_K)
    top_w = jax.nn.softmax(top_val, axis=-1)
    combine = jnp.sum(jax.nn.one_hot(top_idx, N_EXPERTS, dtype=jnp.float32) * top_w[..., None], axis=1)

    def expert(acc, params):
        wgu, bgu, wd, bd, c = params
        gu = t @ wgu + bgu
        gate = jnp.minimum(gu[:, :D_FF], SWIGLU_LIMIT)
        up = jnp.clip(gu[:, D_FF:], -SWIGLU_LIMIT, SWIGLU_LIMIT)
        act = (up + 1.0) * gate * jax.nn.sigmoid(SWIGLU_ALPHA * gate)
        return acc + c[:, None] * (act @ wd + bd), None

    acc, _ = lax.scan(expert, jnp.zeros_like(t), (w_gu, b_gu, w_down, b_down, combine.T.astype(t.dtype)))
    return acc.reshape(B, S, D)


def decoder_layer(h, p_i, w_in, w_cmp1, w_cmp2, pe_cmp, w_br_sb, w_br_nsa, w_o,
                  ln1_g, ln1_b, ln2_g, ln2_b, w_router, b_router, w_gu, b_gu,
                  w_down, b_down, w_ple_gate, w_ple_proj):
    B, S, _ = h.shape
    f32 = jnp.float32
    (sb_q, sb_k, sb_v, nsa_q, k_cmp, v_cmp, k_sel, v_sel, k_win, v_win,
     nsa_gate, g_sb, g_nsa) = split_columns(h @ w_in)

    y_sb = stick_breaking_attention(to_heads(sb_q, SB_HEADS), to_heads(sb_k, SB_HEADS), to_heads(sb_v, SB_HEADS))
    y_sb = y_sb.transpose(0, 2, 1, 3).reshape(B, S, SB_W).astype(h.dtype)

    q_grp = nsa_q.reshape(B, S, NSA_KV_GROUPS, NSA_GROUP, HEAD_DIM).transpose(0, 2, 3, 1, 4).astype(f32)
    gate_grp = nsa_gate.reshape(B, S, NSA_KV_GROUPS, NSA_GROUP, 3).transpose(0, 2, 3, 1, 4).astype(f32)
    y_nsa = native_sparse_attention(
        q_grp,
        to_heads(k_cmp, NSA_KV_GROUPS), to_heads(v_cmp, NSA_KV_GROUPS),
        to_heads(k_sel, NSA_KV_GROUPS), to_heads(v_sel, NSA_KV_GROUPS),
        to_heads(k_win, NSA_KV_GROUPS), to_heads(v_win, NSA_KV_GROUPS),
        gate_grp, w_cmp1.astype(f32), w_cmp2.astype(f32), pe_cmp.astype(f32))
    y_nsa = y_nsa.transpose(0, 3, 1, 2, 4).reshape(B, S, NSA_QW).astype(h.dtype)

    merged = jax.nn.sigmoid(g_sb) * (y_sb @ w_br_sb) + jax.nn.sigmoid(g_nsa) * (y_nsa @ w_br_nsa)
    h1 = layer_norm(DN_ALPHA * h + merged @ w_o, ln1_g, ln1_b)

    ple = jax.nn.sigmoid(h1 @ w_ple_gate) * (p_i @ w_ple_proj)
    moe_out = moe_ffn(h1, w_router, b_router, w_gu, b_gu, w_down, b_down)
    return layer_norm(DN_ALPHA * h1 + moe_out + ple, ln2_g, ln2_b)


def setup_inputs(seed: int = 0) -> dict:
    key = jax.random.key(seed)
    ks = jax.random.split(key, 21)
    f32 = jnp.float32
    L = DEPTH

    def normal(k, shape, std):
        return jax.random.normal(k, shape, f32) * std

    in_scale = jnp.concatenate(
        [jnp.ones((2 * SB_W,), f32), jnp.full((SB_W,), DN_BETA, f32), jnp.ones((NSA_QW,), f32)]
        + [jnp.ones((NSA_KVW,), f32), jnp.full((NSA_KVW,), DN_BETA, f32)] * 3
        + [jnp.ones((NSA_GATE_W + 2 * D_MODEL,), f32)])
    return {
        "x": normal(ks[0], (BATCH, SEQ, D_MODEL), 1.0),
        "p": normal(ks[1], (DEPTH, BATCH, SEQ, PLE_DIM), 1.0),
        "w_in": normal(ks[2], (L, D_MODEL, N_IN), D_MODEL ** -0.5) * in_scale,
        "w_cmp1": normal(ks[3], (L, 2, CMP_BLOCK, HEAD_DIM, CMP_HIDDEN), (CMP_BLOCK * HEAD_DIM) ** -0.5),
        "w_cmp2": normal(ks[4], (L, 2, CMP_HIDDEN, HEAD_DIM), CMP_HIDDEN ** -0.5),
        "pe_cmp": normal(ks[5], (L, 2, CMP_BLOCK, HEAD_DIM), 0.1),
        "w_br_sb": normal(ks[6], (L, SB_W, D_MODEL), DN_BETA * SB_W ** -0.5),
        "w_br_nsa": normal(ks[7], (L, NSA_QW, D_MODEL), DN_BETA * NSA_QW ** -0.5),
        "w_o": normal(ks[8], (L, D_MODEL, D_MODEL), DN_BETA * D_MODEL ** -0.5),
        "ln1_g": 1.0 + normal(ks[9], (L, D_MODEL), 0.02),
        "ln1_b": normal(ks[10], (L, D_MODEL), 0.02),
        "ln2_g": 1.0 + normal(ks[11], (L, D_MODEL), 0.02),
        "ln2_b": normal(ks[12], (L, D_MODEL), 0.02),
        "w_router": normal(ks[13], (L, D_MODEL, N_EXPERTS), D_MODEL ** -0.5),
        "b_router": normal(ks[14], (L, N_EXPERTS), 0.01),
        "w_gu": normal(ks[15], (L, N_EXPERTS, D_MODEL, 2 * D_FF), DN_BETA * D_MODEL ** -0.5),
        "b_gu": normal(ks[16], (L, N_EXPERTS, 2 * D_FF), 0.01),
        "w_down": normal(ks[17], (L, N_EXPERTS, D_FF, D_MODEL), DN_BETA * D_FF ** -0.5),
        "b_down": normal(ks[18], (L, N_EXPERTS, D_MODEL), 0.01),
        "w_ple_gate": normal(ks[19], (L, D_MODEL, D_MODEL), D_MODEL ** -0.5),
        "w_ple_proj": normal(ks[20], (L, PLE_DIM, D_MODEL), DN_BETA * PLE_DIM ** -0.5),
    }


def reference(x, p, w_in, w_cmp1, w_cmp2, pe_cmp, w_br_sb, w_br_nsa, w_o,
              ln1_g, ln1_b, ln2_g, ln2_b, w_router, b_router, w_gu, b_gu,
              w_down, b_down, w_ple_gate, w_ple_proj):
    h = x
    for i in range(DEPTH):
        h = decoder_layer(h, p[i], w_in[i], w_cmp1[i], w_cmp2[i], pe_cmp[i],
                          w_br_sb[i], w_br_nsa[i], w_o[i],
                          ln1_g[i], ln1_b[i], ln2_g[i], ln2_b[i],
                          w_router[i], b_router[i], w_gu[i], b_gu[i],
                          w_down[i], b_down[i], w_ple_gate[i], w_ple_proj[i])
    return h
```

```python
import contextlib
import numpy as np
import ml_dtypes
import concourse.bass as bass
import concourse.mybir as mybir
from concourse.bass_utils import run_bass_kernel_spmd

F32 = mybir.dt.float32
BF16 = mybir.dt.bfloat16
ALU = mybir.AluOpType
AF = mybir.ActivationFunctionType
AX = mybir.AxisListType

S = 4096
D = 1024
NT = S // 128
DEPTH = 4
N_IN = 4888
NE = 32
DFF = 1024
ALPHA = (2 * DEPTH) ** 0.25
EPS = 1e-5
NEG = -30000.0
O_SBQ, O_SBK, O_SBV, O_NQ = 0, 512, 1024, 1536
O_KC, O_VC, O_KS, O_VS, O_KW, O_VW = 2048, 2176, 2304, 2432, 2560, 2688
O_GATE, O_GSB, O_GNSA = 2816, 2840, 3864
SIG_MAX = float(1.0 / (1.0 + np.exp(-1.702 * 7.0)))


class Src:
    __slots__ = ("sem", "val")

    def __init__(self, sem):
        self.sem = sem
        self.val = 0


class Trk:
    __slots__ = ("w", "r", "x")

    def __init__(self, x=False):
        self.w = None
        self.r = {}
        self.x = x


class Eng:
    def __init__(self, name, h, sem):
        self.name = name
        self.h = h
        self.src = Src(sem)
        self.known = {}
        self.slots = []
        self.slot_i = 0


class Prog:
    def __init__(self, nc, es):
        self.nc = nc
        self.es = es
        mk = lambda n, h: Eng(n, h, es.enter_context(nc.semaphore("s_" + n)))
        self.pe = mk("pe", nc.tensor)
        self.act = mk("act", nc.scalar)
        self.dve = mk("dve", nc.vector)
        self.pool = mk("pool", nc.gpsimd)
        self.sp = mk("sp", nc.sync)
        self.engs = [self.pe, self.act, self.dve, self.pool, self.sp]
        for e, n in ((self.sp, 20), (self.pool, 20), (self.act, 6)):
            e.slots = [Src(es.enter_context(nc.semaphore("d_%s%d" % (e.name, i)))) for i in range(n)]
        self.ninst = 0
        self.dead = False

    def _deps(self, E, R, W):
        deps = {}
        for t in R:
            if t.w is not None:
                s, v = t.w
                if deps.get(s, 0) < v:
                    deps[s] = v
        for t in W:
            if t.w is not None:
                s, v = t.w
                if deps.get(s, 0) < v:
                    deps[s] = v
            for s, v in t.r.items():
                if deps.get(s, 0) < v:
                    deps[s] = v
        for s, v in deps.items():
            if s is E.src and E is self.pe:
                continue
            if E.known.get(s, 0) < v:
                E.h.wait_ge(s.sem, v)
                E.known[s] = v
                self.ninst += 1

    def op(self, E, fn, R, W, *a, **kw):
        if self.dead:
            return None
        W = list(W) + [t for t in R if t.x]
        R = [t for t in R if not t.x]
        self._deps(E, R, W)
        ins = getattr(E.h, fn)(*a, **kw)
        E.src.val += 1
        ins.then_inc(E.src.sem, 1)
        self.ninst += 1
        v = E.src.val
        for t in R:
            t.r[E.src] = v
        for t in W:
            t.w = (E.src, v)
            t.r = {}
        return ins

    def dma(self, Q, R, W, out, in_):
        if self.dead:
            return None
        self._deps(Q, R, W)
        sl = Q.slots[Q.slot_i]
        Q.slot_i = (Q.slot_i + 1) % len(Q.slots)
        if Q.known.get(sl, 0) < sl.val:
            Q.h.wait_ge(sl.sem, sl.val)
            Q.known[sl] = sl.val
        ins = Q.h.dma_start(out=out, in_=in_)
        sl.val += 16
        ins.then_inc(sl.sem, 16)
        self.ninst += 1
        for t in R:
            t.r[sl] = sl.val
        for t in W:
            t.w = (sl, sl.val)
            t.r = {}

    def barrier(self):
        srcs = [e.src for e in self.engs]
        for e in self.engs:
            srcs += e.slots
        for E in self.engs:
            for s in srcs:
                if s is E.src or s.val == 0:
                    continue
                if E.known.get(s, 0) < s.val:
                    E.h.wait_ge(s.sem, s.val)
                    E.known[s] = s.val

    def mm(self, R, W, out, lhsT, rhs, start=True, stop=True, sgc=False):
        if sgc:
            return self.op(self.pe, "matmul", R, W, out, lhsT=lhsT, rhs=rhs, start=start, stop=stop,
                           skip_group_check=True)
        return self.op(self.pe, "matmul", R, W, out, lhsT=lhsT, rhs=rhs, start=start, stop=stop)

    def tr(self, R, W, out, in_, ident):
        return self.op(self.pe, "transpose", R, W, out, in_, ident)

    def A(self, R, W, out, in_, func, **kw):
        return self.op(self.act, "activation", R, W, out=out, in_=in_, func=func, **kw)

    def V(self, fn, R, W, **kw):
        return self.op(self.dve, fn, R, W, **kw)

    def G(self, fn, R, W, **kw):
        return self.op(self.pool, fn, R, W, **kw)


class Buf:
    _n = [0]

    def __init__(self, P, name, shape, dtype, n=1, psum=False):
        nc = P.nc
        Buf._n[0] += 1
        name = "%s_%d" % (name, Buf._n[0])
        cm = nc.psum_tensor(name, shape, dtype) if psum else nc.sbuf_tensor(name, shape, dtype)
        self.t = P.es.enter_context(cm)
        self.k = [Trk(x=psum) for _ in range(n)]

    def __getitem__(self, key):
        return self.t[key]


class _Stop(Exception):
    pass


def build(n_layers=DEPTH, debug=False, phases="ABCD", bstop=99):
    nc = bass.Bass("TRN2", target_bir_lowering=False)
    dt_in = lambda name, shape, dt=F32: nc.dram_tensor(name, shape, dt, kind="ExternalInput").ap()
    x = dt_in("x", [S, D])
    p = dt_in("p", [DEPTH, S, 256])
    w_in = dt_in("w_in", [DEPTH, D, N_IN])
    w_cmp1 = dt_in("w_cmp1", [DEPTH, 2, 32, 64, 128])
    w_cmp2 = dt_in("w_cmp2", [DEPTH, 2, 128, 64])
    pe_cmpT = dt_in("pe_cmpT", [DEPTH, 2, 64, 32])
    w_br_sb = dt_in("w_br_sb", [DEPTH, 512, D])
    w_br_nsa = dt_in("w_br_nsa", [DEPTH, 512, D])
    w_o = dt_in("w_o", [DEPTH, D, D])
    ln1_g = dt_in("ln1_g", [DEPTH, D])
    ln1_b = dt_in("ln1_b", [DEPTH, D])
    ln2_g = dt_in("ln2_g", [DEPTH, D])
    ln2_b = dt_in("ln2_b", [DEPTH, D])
    w_router = dt_in("w_router", [DEPTH, D, NE])
    b_router = dt_in("b_router", [DEPTH, NE])
    w_gu = dt_in("w_gu", [DEPTH, NE, D, 2 * DFF])
    b_guT = dt_in("b_guT", [DEPTH, 128, NE, 16])
    w_down = dt_in("w_down", [DEPTH, NE, DFF, D])
    b_down = dt_in("b_down", [DEPTH, NE, D])
    w_ple_gate = dt_in("w_ple_gate", [DEPTH, D, D])
    w_ple_proj = dt_in("w_ple_proj", [DEPTH, 256, D])
    c_identb = dt_in("c_identb", [128, 128], BF16)
    c_identf = dt_in("c_identf", [128, 128], F32)
    c_triui = dt_in("c_triui", [128, 128], BF16)
    c_ones = dt_in("c_ones", [128, 128], BF16)
    c_masksb = dt_in("c_masksb", [128, 4, 512], BF16)
    c_ident4 = dt_in("c_ident4", [128, 4, 128], BF16)
    c_negC = dt_in("c_negC", [128, 128], BF16)
    c_negL = dt_in("c_negL", [128, 128], BF16)
    c_maskcmp = dt_in("c_maskcmp", [128, NT, 128], BF16)
    c_pairsum = dt_in("c_pairsum", [128, 64], BF16)
    c_F = dt_in("c_F", [128, NT, 64], BF16)
    c_V = dt_in("c_V", [128, NT, 64], BF16)
    c_qaug = dt_in("c_qaug", [4, 8, S], BF16)
    c_kaug = dt_in("c_kaug", [4, S], BF16)
    c_kcaug = dt_in("c_kcaug", [4, 128], BF16)

    y = nc.dram_tensor("y", [S, D], F32, kind="ExternalOutput").ap()
    skind = "ExternalOutput" if debug else "Internal"
    hT_d = nc.dram_tensor("hT_d", [128, 8, S], BF16, kind=skind).ap()
    h1T_d = nc.dram_tensor("h1T_d", [128, 8, S], BF16, kind=skind).ap()
    h_tok = nc.dram_tensor("h_tok", [S, D], F32, kind=skind).ap()
    h1_tok = nc.dram_tensor("h1_tok", [S, D], F32, kind=skind).ap()
    ysbT_d = nc.dram_tensor("ysbT_d", [128, 4, S], BF16, kind=skind).ap()
    ynsaT_d = nc.dram_tensor("ynsaT_d", [128, 4, S], BF16, kind=skind).ap()

    top = contextlib.ExitStack()
    P = Prog(nc, top)
    PE, ACT, DVE, POOL, SP = P.pe, P.act, P.dve, P.pool, P.sp

    identb = Buf(P, "identb", [128, 128], BF16)
    identf = Buf(P, "identf", [128, 128], F32)
    P.dma(SP, [], identb.k, identb[:], c_identb)
    P.dma(SP, [], identf.k, identf[:], c_identf)

    rr = [0]

    def chk(n):
        if n >= bstop:
            P.dead = True

    def evac(R, W, out, in_, scale=None):
        rr[0] ^= 1
        if rr[0]:
            if scale is None:
                P.A(R, W, out, in_, AF.Copy)
            else:
                P.A(R, W, out, in_, AF.Copy, scale=scale)
        else:
            if scale is None:
                P.V("tensor_copy", R, W, out=out, in_=in_)
            else:
                P.V("tensor_scalar", R, W, out=out, in0=in_, scalar1=scale, scalar2=None, op0=ALU.mult)

    def win_cols(l, c0, w):
        return w_in[l].rearrange("(kc k) n -> k kc n", k=128)[:, :, c0:c0 + w]

    def tok_to_T(es, src_tok, dstT_d, name):
        xt = [Buf(P, name + "xt%d" % i, [128, D], BF16) for i in range(2)]
        st = [Buf(P, name + "st%d" % i, [128, 8, 512], BF16) for i in range(2)]
        pst = [Buf(P, name + "ps%d" % i, [128, 8, 128], BF16, psum=True) for i in range(2)]
        for tt in range(NT):
            b = xt[tt % 2]
            P.dma(POOL, [], b.k, b[:], src_tok[tt * 128:(tt + 1) * 128, :])
            ps = pst[tt % 2]
            for kc in range(8):
                P.tr(b.k + identb.k, ps.k, ps[:, kc, :], b[:, kc * 128:(kc + 1) * 128], identb[:])
            sb = st[(tt // 4) % 2]
            evac(ps.k, sb.k, sb[:, :, (tt % 4) * 128:(tt % 4 + 1) * 128], ps[:, :, :])
            if tt % 4 == 3:
                ch = tt // 4
                P.dma(SP, sb.k, [], dstT_d[:, :, ch * 512:(ch + 1) * 512], sb[:])

    def phase_sb(l, XT):
        with contextlib.ExitStack() as es:
            P.es = es
            triui = Buf(P, "triui", [128, 128], BF16)
            ones = Buf(P, "ones", [128, 128], BF16)
            masksb = Buf(P, "masksb", [128, 4, 512], BF16)
            P.dma(SP, [], triui.k, triui[:], c_triui)
            P.dma(SP, [], ones.k, ones[:], c_ones)
            P.dma(SP, [], masksb.k, masksb[:], c_masksb)
            W3 = [Buf(P, "W3_%d" % i, [128, 8, 384], BF16) for i in range(2)]
            qT = [Buf(P, "qT%d" % i, [128, S], BF16) for i in range(2)]
            kT = [Buf(P, "kT%d" % i, [128, S], BF16) for i in range(2)]
            kN = [Buf(P, "kN%d" % i, [128, S], BF16) for i in range(2)]
            Vt = [Buf(P, "Vt%d" % i, [128, NT, 128], BF16) for i in range(2)]
            yT = [Buf(P, "yT%d" % i, [128, S], BF16) for i in range(2)]
            NB = 6
            psz = [Buf(P, "psz%d" % i, [128, 512], F32, psum=True) for i in range(NB)]
            pso = [Buf(P, "pso%d" % i, [128, 512], F32, psum=True) for i in range(2)]
            psp = psz[0:2]
            ez = [Buf(P, "ez%d" % i, [128, 512], F32) for i in range(NB)]
            spb = [Buf(P, "spb%d" % i, [128, 512], BF16) for i in range(NB)]
            wTb = [Buf(P, "wTb%d" % i, [128, 512], BF16) for i in range(NB)]
            csb = [Buf(P, "csb%d" % i, [128, 512], BF16) for i in range(2)]
            for hp in range(4):
                w3 = W3[hp % 2]
                for j, c0 in enumerate((O_SBQ, O_SBK, O_SBV)):
                    P.dma(POOL, [], w3.k, w3[:, :, j * 128:(j + 1) * 128], win_cols(l, c0 + hp * 128, 128))
                q, k, kn, v, yt = qT[hp % 2], kT[hp % 2], kN[hp % 2], Vt[hp % 2], yT[hp % 2]
                for j in range(2):
                    for ch in range(8):
                        ps = psp[ch % 2]
                        csl = slice(ch * 512, (ch + 1) * 512)
                        for kc in range(8):
                            P.mm(w3.k + XT.k, ps.k, ps[:, :], w3[:, kc, j * 128:(j + 1) * 128],
                                 XT[:, kc, csl], start=(kc == 0), stop=(kc == 7))
                        if j == 0:
                            P.A(ps.k, q.k, q[:, csl], ps[:, :], AF.Copy, scale=0.125)
                        else:
                            P.A(ps.k, k.k, k[:, csl], ps[:, :], AF.Copy)
                            P.V("tensor_scalar", ps.k, kn.k, out=kn[:, csl], in0=ps[:, :], scalar1=-1.0, scalar2=None,
                                op0=ALU.mult)
                for tt in range(NT):
                    ps = psp[tt % 2]
                    for kc in range(8):
                        P.mm(w3.k + XT.k, ps.k, ps[:, 0:128], XT[:, kc, tt * 128:(tt + 1) * 128],
                             w3[:, kc, 256:384], start=(kc == 0), stop=(kc == 7))
                    evac(ps.k, v.k, v[:, tt, :], ps[:, 0:128])
                items = []
                for hh in range(2):
                    for qt in range(8):
                        kbs = list(range(4 * qt + 3, -1, -1))
                        for n, kb in enumerate(kbs):
                            items.append((hh, qt, kb, n == 0, n == len(kbs) - 1))
                cs_i = [0]

                def stage1(i):
                    hh, qt, kb, first, last = items[i]
                    b0 = 64 * hh
                    ib = i % NB
                    pz, e_, s_ = psz[ib], ez[ib], spb[ib]
                    P.mm(kn.k + q.k, pz.k, pz[:, :], kn[b0:b0 + 64, kb * 128:(kb + 1) * 128],
                         q[b0:b0 + 64, qt * 512:(qt + 1) * 512], start=True, stop=False, sgc=True)
                    P.A(pz.k, e_.k, e_[:], pz[:, :], AF.Exp, scale=-1.0)
                    P.A(e_.k, s_.k, s_[:], e_[:], AF.Ln, bias=1.0)
                    if kb >= 4 * qt:
                        P.V("tensor_tensor", s_.k + masksb.k, s_.k, out=s_[:], in0=s_[:],
                            in1=masksb[:, kb - 4 * qt, :], op=ALU.mult)

                def stage2a(i):
                    hh, qt, kb, first, last = items[i]
                    ib = i % NB
                    pc, s_, w_ = psz[ib], spb[ib], wTb[ib]
                    P.mm(triui.k + s_.k, pc.k, pc[:, :], triui[:], s_[:], start=False, stop=first, sgc=True)
                    if not first:
                        cb = csb[cs_i[0] % 2]
                        P.mm(ones.k + cb.k, pc.k, pc[:, :], ones[:], cb[:], start=False, stop=True, sgc=True)
                    if not last:
                        if first:
                            cs_i[0] += 1
                            cb = csb[cs_i[0] % 2]
                            P.V("tensor_copy", s_.k, cb.k, out=cb[:], in_=s_[:])
                        else:
                            cb0 = csb[cs_i[0] % 2]
                            cs_i[0] += 1
                            cb = csb[cs_i[0] % 2]
                            P.V("tensor_tensor", s_.k + cb0.k, cb.k, out=cb[:], in0=cb0[:], in1=s_[:], op=ALU.add)
                    P.A(pc.k, w_.k, w_[:], pc[:, :], AF.Exp, scale=-1.0)
                    if kb >= 4 * qt:
                        P.V("tensor_tensor", w_.k + masksb.k, w_.k, out=w_[:], in0=w_[:],
                            in1=masksb[:, kb - 4 * qt, :], op=ALU.mult)

                def stage2b(i):
                    hh, qt, kb, first, last = items[i]
                    b0 = 64 * hh
                    w_ = wTb[i % NB]
                    po = pso[(hh * 8 + qt) % 2]
                    P.mm(v.k + w_.k, po.k, po[b0:b0 + 64, :], v[:, kb, b0:b0 + 64], w_[:], start=first, stop=last)
                    if last:
                        evac(po.k, yt.k, yt[b0:b0 + 64, qt * 512:(qt + 1) * 512], po[b0:b0 + 64, :])

                LOOK = NB - 1
                n_it = len(items)
                for i in range(min(LOOK, n_it)):
                    stage1(i)
                stage2a(0)
                for i in range(n_it):
                    if i + LOOK < n_it:
                        stage1(i + LOOK)
                    if i + 1 < n_it:
                        stage2a(i + 1)
                    stage2b(i)
                P.dma(SP, yt.k, [], ysbT_d[:, hp, :], yt[:])
            P.barrier()
        P.es = top

    def phase_nsa(l, XT):
        with contextlib.ExitStack() as es:
            P.es = es
            ident4 = Buf(P, "ident4", [128, 4, 128], BF16)
            negC = Buf(P, "negC", [128, 128], BF16)
            negL = Buf(P, "negL", [128, 128], BF16)
            maskcmp = Buf(P, "maskcmp", [128, NT, 128], BF16)
            Ft = Buf(P, "Ft", [128, NT, 64], BF16)
            Vt_ = Buf(P, "Vt_", [128, NT, 64], BF16)
            for b, c in ((ident4, c_ident4), (negC, c_negC), (negL, c_negL), (maskcmp, c_maskcmp), (Ft, c_F), (Vt_, c_V)):
                P.dma(SP, [], b.k, b[:], c)
            Qa = Buf(P, "Qa", [68, 4, S], BF16)
            kwa = Buf(P, "kwa", [68, S], BF16)
            ksa = Buf(P, "ksa", [68, S], BF16)
            kca = Buf(P, "kca", [68, 128], BF16)
            vsa = Buf(P, "vsa", [128, NT, 68], BF16)
            vwa = Buf(P, "vwa", [128, NT, 68], BF16)
            rhsc = Buf(P, "rhsc", [128, 128], BF16)
            gates = Buf(P, "gates", [128, NT, 24], F32)
            Wg = Buf(P, "Wg", [128, 8, 24], BF16)
            NPS = 4
            pss = [Buf(P, "pss%d" % i, [128, 512], F32, psum=True) for i in range(NPS)]
            psoc = [Buf(P, "psoc%d" % i, [128, 4, 128], F32, psum=True) for i in range(1)]
            psos_b = Buf(P, "psos", [128, 512], F32, psum=True)
            psow_b = Buf(P, "psow", [128, 512], F32, psum=True)
            pst_b = Buf(P, "pstn", [128, 8, 128], BF16, psum=True)

            class _View:
                def __init__(self, b, ap):
                    self.k = b.k
                    self.ap = ap

                def __getitem__(self, key):
                    return self.ap[key]

            psos = _View(psos_b, psos_b[:, 0:260].rearrange("t (r d) -> t r d", r=4))
            psow = _View(psow_b, psow_b[:, 0:260].rearrange("t (r d) -> t r d", r=4))
            pst = _View(pst_b, pst_b[:, 0:2, :])
            psp = pss
            P.dma(SP, [], kwa.k, kwa[64:68, :], c_kaug)
            P.dma(SP, [], ksa.k, ksa[64:68, :], c_kaug)
            P.dma(SP, [], kca.k, kca[64:68, :], c_kcaug)
            P.dma(SP, [], rhsc.k, rhsc[:, 64:128], c_pairsum)
            P.G("memset", [], vsa.k, ap=vsa[:, :, 64:65], constant=1.0)
            P.G("memset", [], vwa.k, ap=vwa[:, :, 64:65], constant=1.0)
            P.dma(POOL, [], Wg.k, Wg[:], win_cols(l, O_GATE, 24))
            for tt in range(NT):
                ps = psp[tt % 2]
                for kc in range(8):
                    P.mm(Wg.k + XT.k, ps.k, ps[:, 0:24], XT[:, kc, tt * 128:(tt + 1) * 128], Wg[:, kc, :],
                         start=(kc == 0), stop=(kc == 7))
                P.A(ps.k, gates.k, gates[:, tt, :], ps[:, 0:24], AF.Sigmoid)
            it = 0
            chk(1)
            for g in range(2):
              with contextlib.ExitStack() as esp:
                P.es = esp
                kcl = [Buf(P, "kcl%d" % i, [64, 32, 128], BF16) for i in range(2)]
                w1 = [Buf(P, "w1_%d" % i, [64, 32, 128], BF16) for i in range(2)]
                w2 = [Buf(P, "w2_%d" % i, [128, 64], BF16) for i in range(2)]
                peT = [Buf(P, "peT%d" % i, [64, 32], F32) for i in range(2)]
                hid = [Buf(P, "hid%d" % i, [128, 128], BF16) for i in range(2)]
                sg = Buf(P, "sg", [128, 128], F32)
                Wq = Buf(P, "Wq", [128, 8, 256], BF16)
                Wk = Buf(P, "Wk", [128, 8, 256], BF16)
                Wv = Buf(P, "Wv", [128, 8, 128], BF16)
                P.dma(SP, [], Qa.k, Qa[64:68, :, :], c_qaug[:, 4 * g:4 * g + 4, :])
                P.dma(POOL, [], Wq.k, Wq[:], win_cols(l, O_NQ + 256 * g, 256))
                for j, c0 in enumerate((O_KC, O_VC, O_KS, O_KW)):
                    P.dma(POOL, [], Wk.k, Wk[:, :, j * 64:(j + 1) * 64], win_cols(l, c0 + 64 * g, 64))
                for j, c0 in enumerate((O_VS, O_VW)):
                    P.dma(POOL, [], Wv.k, Wv[:, :, j * 64:(j + 1) * 64], win_cols(l, c0 + 64 * g, 64))
                for j in range(2):
                    P.dma(POOL, [], w1[j].k, w1[j][:], w_cmp1[l, j].rearrange("l d h -> d l h"))
                    P.dma(POOL, [], w2[j].k, w2[j][:], w_cmp2[l, j])
                    P.dma(SP, [], peT[j].k, peT[j][:], pe_cmpT[l, j])
                for ch in range(8):
                    tsl = slice(ch * 512, (ch + 1) * 512)
                    for r in range(4):
                        ps = psp[it % NPS]; it += 1
                        for kc in range(8):
                            P.mm(Wq.k + XT.k, ps.k, ps[0:64, :], Wq[:, kc, r * 64:(r + 1) * 64], XT[:, kc, tsl],
                                 start=(kc == 0), stop=(kc == 7))
                        evac(ps.k, Qa.k, Qa[0:64, r, tsl], ps[0:64, :], scale=0.125)
                    for j in range(4):
                        ps = psp[it % NPS]; it += 1
                        for kc in range(8):
                            P.mm(Wk.k + XT.k, ps.k, ps[0:64, :], Wk[:, kc, j * 64:(j + 1) * 64], XT[:, kc, tsl],
                                 start=(kc == 0), stop=(kc == 7))
                        if j < 2:
                            dst = kcl[j]
                            evac(ps.k, dst.k, dst[:, :, ch * 16:(ch + 1) * 16],
                                 ps[0:64, :].rearrange("d (c l) -> d l c", l=32))
                        else:
                            dst = ksa if j == 2 else kwa
                            evac(ps.k, dst.k, dst[0:64, tsl], ps[0:64, :])
                chk(2)
                for tt in range(NT):
                    ps = psp[it % NPS]; it += 1
                    for kc in range(8):
                        P.mm(Wv.k + XT.k, ps.k, ps[:, 0:128], XT[:, kc, tt * 128:(tt + 1) * 128], Wv[:, kc, :],
                             start=(kc == 0), stop=(kc == 7))
                    evac(ps.k, vsa.k, vsa[:, tt, 0:64], ps[:, 0:64])
                    evac(ps.k, vwa.k, vwa[:, tt, 0:64], ps[:, 64:128])
                chk(3)
                for j in range(2):
                    P.V("tensor_tensor", kcl[j].k + peT[j].k, kcl[j].k, out=kcl[j][:], in0=kcl[j][:],
                        in1=peT[j][:, :].unsqueeze(2).to_broadcast([64, 32, 128]), op=ALU.add)
                    ps = psp[it % NPS]; it += 1
                    for li in range(32):
                        P.mm(w1[j].k + kcl[j].k, ps.k, ps[:, 0:128], w1[j][:, li, :], kcl[j][:, li, :],
                             start=(li == 0), stop=(li == 31))
                    P.A(ps.k, sg.k, sg[:], ps[:, 0:128], AF.Sigmoid)
                    P.V("tensor_tensor", sg.k + ps.k, hid[j].k, out=hid[j][:], in0=sg[:], in1=ps[:, 0:128], op=ALU.mult)
                    ps2 = psp[it % NPS]; it += 1
                    if j == 0:
                        P.mm(w2[j].k + hid[j].k, ps2.k, ps2[0:64, 0:128], w2[j][:], hid[j][:])
                        evac(ps2.k, kca.k, kca[0:64, :], ps2[0:64, 0:128])
                    else:
                        P.mm(w2[j].k + hid[j].k, ps2.k, ps2[:, 0:64], hid[j][:], w2[j][:])
                        evac(ps2.k, rhsc.k, rhsc[:, 0:64], ps2[:, 0:64])
                chk(4)
                P.barrier()
              with contextlib.ExitStack() as esm:
                P.es = esm
                sm = [dict((n, Buf(P, "%s%d" % (n, i), sh, F32)) for n, sh in
                           (("Z", [128, 12]), ("ri", [128, 12]), ("cf", [128, 12]), ("imp", [128, 64]),
                            ("m8", [128, 16]), ("imp2", [128, 64]), ("oc", [128, 4, 64]), ("t1", [128, 4, 64]),
                            ("t2", [128, 4, 64]))) for i in range(2)]
                nsel = [Buf(P, "nsel%d" % i, [128, 64], BF16) for i in range(2)]
                ytok = [Buf(P, "ytok%d" % i, [128, 256], BF16) for i in range(2)]
                yst = [Buf(P, "yst%d" % i, [128, 2, 512], BF16) for i in range(2)]
                nsx = [Buf(P, "nsx%d" % i, [128, S], BF16) for i in range(2)]
                eb = [Buf(P, "eb%d" % i, [128, 4, 128], BF16) for i in range(NPS)]
                def sel_stage(tt):
                    nonlocal it
                    s_ = sm[tt % 2]
                    tq = slice(tt * 128, (tt + 1) * 128)
                    Mc = 4 * (tt + 1)
                    pco = psoc[0]
                    ps = pss[it % NPS]; e_ = eb[it % NPS]; it += 1
                    P.mm(kca.k + Qa.k, ps.k, ps[0:Mc, :], kca[:, 0:Mc], Qa[:, :, tq])
                    P.A(ps.k, e_.k, e_[0:Mc, :, :], ps[0:Mc, :].rearrange("c (r t) -> c r t", r=4), AF.Exp)
                    P.V("tensor_tensor", e_.k + maskcmp.k, e_.k, out=e_[0:Mc, :, :], in0=e_[0:Mc, :, :],
                        in1=maskcmp[0:Mc, tt:tt + 1, :].to_broadcast([Mc, 4, 128]), op=ALU.mult)
                    for r in range(4):
                        P.mm(e_.k + rhsc.k, pco.k, pco[:, r, :], e_[0:Mc, r, :], rhsc[0:Mc, :])
                    Z, ri, cf = s_["Z"], s_["ri"], s_["cf"]
                    P.V("tensor_reduce", pco.k, Z.k, out=Z[:, 0:4], in_=pco[:, :, 64:128], axis=AX.X, op=ALU.add)
                    P.V("tensor_scalar", Z.k, Z.k, out=Z[:, 0:4], in0=Z[:, 0:4], scalar1=1e-30, scalar2=None, op0=ALU.max)
                    P.V("reciprocal", Z.k, ri.k, out=ri[:, 0:4], in_=Z[:, 0:4])
                    imp = s_["imp"]
                    P.V("tensor_scalar", pco.k + ri.k, imp.k, out=imp[:], in0=pco[:, 0, 64:128], scalar1=ri[:, 0:1],
                        scalar2=None, op0=ALU.mult)
                    for r in range(1, 4):
                        P.V("scalar_tensor_tensor", pco.k + ri.k + imp.k, imp.k, out=imp[:], in0=pco[:, r, 64:128],
                            scalar=ri[:, r:r + 1], in1=imp[:], op0=ALU.mult, op1=ALU.add)
                    oc = s_["oc"]
                    P.A(pco.k, oc.k, oc[:], pco[:, :, 0:64], AF.Copy)
                    P.V("tensor_tensor", imp.k + Ft.k, imp.k, out=imp[:], in0=imp[:], in1=Ft[:, tt, :], op=ALU.max)
                    P.V("tensor_tensor", imp.k + Vt_.k, imp.k, out=imp[:], in0=imp[:], in1=Vt_[:, tt, :], op=ALU.add)
                    m8, imp2 = s_["m8"], s_["imp2"]
                    P.V("max", imp.k, m8.k, out=m8[:, 0:8], in_=imp[:])
                    P.V("match_replace", m8.k + imp.k, imp2.k, out=imp2[:], in_to_replace=m8[:, 0:8], in_values=imp[:],
                        imm_value=-3.0e38)
                    P.V("max", imp2.k, m8.k, out=m8[:, 8:16], in_=imp2[:])
                    ns = nsel[tt % 2]
                    P.V("tensor_scalar", imp.k + m8.k, ns.k, out=ns[:], in0=imp[:], scalar1=m8[:, 15:16], scalar2=NEG,
                        op0=ALU.is_lt, op1=ALU.mult)
                    nx = nsx[tt % 2]
                    nkeys = (tt + 1) * 128
                    P.V("tensor_copy", ns.k, nx.k, out=nx[:, 0:nkeys].rearrange("t (b s) -> t b s", s=64),
                        in_=ns[:, 0:2 * (tt + 1)].unsqueeze(2).to_broadcast([128, 2 * (tt + 1), 64]))

                def attn_stage(tt):
                    nonlocal it
                    s_ = sm[tt % 2]
                    tq = slice(tt * 128, (tt + 1) * 128)
                    nx = nsx[tt % 2]
                    Z, ri, cf, oc = s_["Z"], s_["ri"], s_["cf"], s_["oc"]
                    kb0 = max(0, tt - 4)
                    its = [("s", kb) for kb in range(tt + 1)] + [("w", kb) for kb in range(kb0, tt + 1)]
                    base = it
                    it += len(its)

                    def st1(n):
                        br, kb = its[n]
                        ps = pss[(base + n) % NPS]; e_ = eb[(base + n) % NPS]
                        ks_ = slice(kb * 128, (kb + 1) * 128)
                        if br == "s":
                            P.mm(ksa.k + Qa.k, ps.k, ps[:, :], ksa[:, ks_], Qa[:, :, tq], start=True, stop=False)
                            P.mm(nx.k + ident4.k, ps.k, ps[:, :], nx[:, ks_], ident4[:, :, :], start=False, stop=(kb != tt))
                            if kb == tt:
                                P.mm(negC.k + ident4.k, ps.k, ps[:, :], negC[:], ident4[:, :, :], start=False, stop=True)
                        else:
                            lowm = (kb == tt - 4)
                            P.mm(kwa.k + Qa.k, ps.k, ps[:, :], kwa[:, ks_], Qa[:, :, tq], start=True,
                                 stop=not (lowm or kb == tt))
                            if kb == tt:
                                P.mm(negC.k + ident4.k, ps.k, ps[:, :], negC[:], ident4[:, :, :], start=False, stop=True)
                            if lowm:
                                P.mm(negL.k + ident4.k, ps.k, ps[:, :], negL[:], ident4[:, :, :], start=False, stop=True)
                        P.A(ps.k, e_.k, e_[:, :, :], ps[:, :].rearrange("c (r t) -> c r t", r=4), AF.Exp)

                    def st2(n):
                        br, kb = its[n]
                        e_ = eb[(base + n) % NPS]
                        for r in range(4):
                            if br == "s":
                                P.mm(e_.k + vsa.k, psos.k, psos[:, r, :], e_[:, r, :], vsa[:, kb, 0:65],
                                     start=(kb == 0 and r == 0), stop=(kb == tt and r == 3), sgc=True)
                            else:
                                P.mm(e_.k + vwa.k, psow.k, psow[:, r, :], e_[:, r, :], vwa[:, kb, 0:65],
                                     start=(kb == kb0 and r == 0), stop=(kb == tt and r == 3), sgc=True)

                    LK = NPS - 1
                    for n in range(min(LK, len(its))):
                        st1(n)
                    for n in range(len(its)):
                        if n + LK < len(its):
                            st1(n + LK)
                        st2(n)
                    P.V("tensor_copy", psos.k, Z.k, out=Z[:, 4:8], in_=psos[:, :, 64])
                    P.V("tensor_copy", psow.k, Z.k, out=Z[:, 8:12], in_=psow[:, :, 64])
                    P.V("reciprocal", Z.k, ri.k, out=ri[:, 4:12], in_=Z[:, 4:12])
                    gv = gates[:, tt, 12 * g:12 * g + 12].rearrange("t (r b) -> t b r", b=3)
                    P.V("tensor_tensor", ri.k + gates.k, cf.k, out=cf[:, :].rearrange("t (b r) -> t b r", r=4),
                        in0=ri[:, :].rearrange("t (b r) -> t b r", r=4), in1=gv, op=ALU.mult)
                    t1, t2 = s_["t1"], s_["t2"]
                    bc = lambda b: cf[:, 4 * b:4 * b + 4].unsqueeze(2).to_broadcast([128, 4, 64])
                    P.V("tensor_tensor", oc.k + cf.k, t1.k, out=t1[:], in0=oc[:], in1=bc(0), op=ALU.mult)
                    P.V("tensor_tensor", psos.k + cf.k, t2.k, out=t2[:], in0=psos[:, :, 0:64], in1=bc(1), op=ALU.mult)
                    P.G("tensor_tensor", t1.k + t2.k, t1.k, out=t1[:], in0=t1[:], in1=t2[:], op=ALU.add)
                    P.V("tensor_tensor", psow.k + cf.k, t2.k, out=t2[:], in0=psow[:, :, 0:64], in1=bc(2), op=ALU.mult)
                    yk = ytok[tt % 2]
                    P.G("tensor_tensor", t1.k + t2.k, yk.k, out=yk[:, :].rearrange("t (r d) -> t r d", r=4), in0=t1[:],
                        in1=t2[:], op=ALU.add)
                    for hf in range(2):
                        P.tr(yk.k + identb.k, pst.k, pst[:, hf, :], yk[:, hf * 128:(hf + 1) * 128], identb[:])
                    ys = yst[(tt // 4) % 2]
                    evac(pst.k, ys.k, ys[:, :, (tt % 4) * 128:(tt % 4 + 1) * 128], pst[:, :, :])
                    if tt % 4 == 3:
                        ch = tt // 4
                        P.dma(SP, ys.k, [], ynsaT_d[:, 2 * g:2 * g + 2, ch * 512:(ch + 1) * 512], ys[:])

                sel_stage(0)
                for tt in range(NT):
                    if tt + 1 < NT:
                        sel_stage(tt + 1)
                    attn_stage(tt)
                P.barrier()
              P.es = es
            P.barrier()
        P.es = top

    def layer_norm(pre, gt, bt, st6, mv, out):
        for hf in range(2):
            P.V("bn_stats", pre.k, st6.k, out=st6[:, hf * 6:(hf + 1) * 6], in_=pre[:, hf * 512:(hf + 1) * 512])
        P.V("bn_aggr", st6.k, mv.k, out=mv[:, 0:2], in_=st6[:, :])
        P.V("tensor_scalar", mv.k, mv.k, out=mv[:, 2:3], in0=mv[:, 1:2], scalar1=EPS, scalar2=None, op0=ALU.add)
        P.A(mv.k, mv.k, mv[:, 2:3], mv[:, 2:3], AF.Sqrt)
        P.V("reciprocal", mv.k, mv.k, out=mv[:, 3:4], in_=mv[:, 2:3])
        P.V("tensor_scalar", pre.k + mv.k, pre.k, out=pre[:], in0=pre[:], scalar1=mv[:, 0:1], scalar2=mv[:, 3:4],
            op0=ALU.subtract, op1=ALU.mult)
        P.G("tensor_tensor", pre.k + gt.k, pre.k, out=pre[:], in0=pre[:], in1=gt[:], op=ALU.mult)
        P.G("tensor_tensor", pre.k + bt.k, out.k, out=out[:], in0=pre[:], in1=bt[:], op=ALU.add)

    def phase_merge(l, res_tok):
        with contextlib.ExitStack() as es:
            P.es = es
            xcs = [Buf(P, "xc%d" % i, [128, 8, 512], BF16) for i in range(2)]
            Wg = Buf(P, "WgC", [128, 8, 2048], BF16)
            Wbr = Buf(P, "Wbr", [128, 8, D], BF16)
            Wo = Buf(P, "Wo", [128, 8, D], BF16)
            gt = Buf(P, "g1", [128, D], F32)
            bt = Buf(P, "b1", [128, D], F32)
            P.dma(POOL, [], Wg.k, Wg[:, :, 0:1024], win_cols(l, O_GSB, 1024))
            P.dma(POOL, [], Wg.k, Wg[:, :, 1024:2048], win_cols(l, O_GNSA, 1024))
            P.dma(POOL, [], Wbr.k, Wbr[:, 0:4, :], w_br_sb[l].rearrange("(kc k) n -> k kc n", k=128))
            P.dma(POOL, [], Wbr.k, Wbr[:, 4:8, :], w_br_nsa[l].rearrange("(kc k) n -> k kc n", k=128))
            P.dma(POOL, [], Wo.k, Wo[:], w_o[l].rearrange("(kc k) n -> k kc n", k=128))
            P.dma(SP, [], gt.k, gt[:], ln1_g[l:l + 1, :].to_broadcast([128, D]))
            P.dma(SP, [], bt.k, bt[:], ln1_b[l:l + 1, :].to_broadcast([128, D]))
            yin = [Buf(P, "yin%d" % i, [128, 8, 512], BF16) for i in range(2)]
            mg = [Buf(P, "mg%d" % i, [128, 8, 512], BF16) for i in range(2)]
            sgs = [Buf(P, "sgs%d" % i, [128, 512], F32) for i in range(2)]
            sgn = [Buf(P, "sgn%d" % i, [128, 512], F32) for i in range(2)]
            ht = [Buf(P, "ht%d" % i, [128, D], F32) for i in range(2)]
            pre = [Buf(P, "pre%d" % i, [128, D], F32) for i in range(2)]
            h1 = [Buf(P, "h1_%d" % i, [128, D], F32) for i in range(2)]
            h1b = [Buf(P, "h1b%d" % i, [128, D], BF16) for i in range(2)]
            st6 = [Buf(P, "st6_%d" % i, [128, 12], F32) for i in range(2)]
            mv = [Buf(P, "mv%d" % i, [128, 4], F32) for i in range(2)]
            stg = [Buf(P, "stg%d" % i, [128, 8, 512], BF16) for i in range(2)]
            ps4 = [Buf(P, "psC%d" % i, [128, 512], F32, psum=True) for i in range(4)]
            pso = [Buf(P, "psCo%d" % i, [128, 2, 512], F32, psum=True) for i in range(1)]
            pst = [Buf(P, "psCt%d" % i, [128, 8, 128], BF16, psum=True) for i in range(2)]
            it = 0
            for ch in range(8):
                tsl = slice(ch * 512, (ch + 1) * 512)
                yi, m_ = yin[ch % 2], mg[ch % 2]
                xc = xcs[ch % 2]
                P.dma(SP, [], xc.k, xc[:], hT_d[:, :, tsl])
                P.dma(SP, [], yi.k, yi[:, 0:4, :], ysbT_d[:, :, tsl])
                P.dma(SP, [], yi.k, yi[:, 4:8, :], ynsaT_d[:, :, tsl])
                for fc in range(8):
                    i2 = it % 2; it += 1
                    pa, pb, pc, pd = ps4
                    fs = slice(fc * 128, (fc + 1) * 128)
                    for kc in range(8):
                        P.mm(Wg.k + xc.k, pa.k, pa[:, :], Wg[:, kc, fs], xc[:, kc, :], start=(kc == 0), stop=(kc == 7))
                    for kc in range(8):
                        P.mm(Wg.k + xc.k, pb.k, pb[:, :], Wg[:, kc, 1024 + fc * 128:1024 + (fc + 1) * 128], xc[:, kc, :],
                             start=(kc == 0), stop=(kc == 7))
                    for kc in range(4):
                        P.mm(Wbr.k + yi.k, pc.k, pc[:, :], Wbr[:, kc, fs], yi[:, kc, :], start=(kc == 0), stop=(kc == 3))
                    for kc in range(4):
                        P.mm(Wbr.k + yi.k, pd.k, pd[:, :], Wbr[:, 4 + kc, fs], yi[:, 4 + kc, :], start=(kc == 0), stop=(kc == 3))
                    a_, b_ = sgs[i2], sgn[i2]
                    P.A(pa.k, a_.k, a_[:], pa[:, :], AF.Sigmoid)
                    P.A(pb.k, b_.k, b_[:], pb[:, :], AF.Sigmoid)
                    P.V("tensor_tensor", a_.k + pc.k, a_.k, out=a_[:], in0=a_[:], in1=pc[:, :], op=ALU.mult)
                    P.V("tensor_tensor", b_.k + pd.k, b_.k, out=b_[:], in0=b_[:], in1=pd[:, :], op=ALU.mult)
                    P.G("tensor_tensor", a_.k + b_.k, m_.k, out=m_[:, fc, :], in0=a_[:], in1=b_[:], op=ALU.add)
                sg_ = stg[ch % 2]
                for t4 in range(4):
                    tt = ch * 4 + t4
                    i2 = tt % 2
                    h_, pr, h1_, hb = ht[i2], pre[i2], h1[i2], h1b[i2]
                    P.dma(SP, [], h_.k, h_[:], res_tok[tt * 128:(tt + 1) * 128, :])
                    po = pso[0]
                    for hf in range(2):
                        for kc in range(8):
                            P.mm(m_.k + Wo.k, po.k, po[:, hf, :], m_[:, kc, t4 * 128:(t4 + 1) * 128],
                                 Wo[:, kc, hf * 512:(hf + 1) * 512], start=(kc == 0), stop=(kc == 7))
                    P.V("scalar_tensor_tensor", h_.k + po.k, pr.k, out=pr[:, :].rearrange("t (a b) -> t a b", a=2),
                        in0=h_[:, :].rearrange("t (a b) -> t a b", a=2), scalar=ALPHA, in1=po[:, :, :],
                        op0=ALU.mult, op1=ALU.add)
                    layer_norm(pr, gt, bt, st6[i2], mv[i2], h1_)
                    P.dma(SP, h1_.k, [], h1_tok[tt * 128:(tt + 1) * 128, :], h1_[:])
                    P.A(h1_.k, hb.k, hb[:], h1_[:], AF.Copy)
                    pt = pst[i2]
                    for kc in range(8):
                        P.tr(hb.k + identb.k, pt.k, pt[:, kc, :], hb[:, kc * 128:(kc + 1) * 128], identb[:])
                    evac(pt.k, sg_.k, sg_[:, :, t4 * 128:(t4 + 1) * 128], pt[:, :, :])
                P.dma(SP, sg_.k, [], h1T_d[:, :, tsl], sg_[:])
            P.barrier()
        P.es = top

    def phase_moe(l, out_tok, outT_d):
        with contextlib.ExitStack() as es:
            P.es = es
            wr = Buf(P, "wr", [128, 8, NE], F32)
            brt = Buf(P, "brt", [128, NE], F32)
            bd = Buf(P, "bd", [NE, D], F32)
            bgu = Buf(P, "bgu", [128, NE, 16], F32)
            bgs = Buf(P, "bgs", [128, NE, 8], F32)
            bgu1 = Buf(P, "bgu1", [128, NE, 8], F32)
            P.dma(SP, [], wr.k, wr[:], w_router[l].rearrange("(kc k) n -> k kc n", k=128))
            P.dma(SP, [], brt.k, brt[:], b_router[l:l + 1, :].to_broadcast([128, NE]))
            P.dma(SP, [], bd.k, bd[:], b_down[l])
            P.dma(SP, [], bgu.k, bgu[:], b_guT[l])
            P.V("tensor_scalar", bgu.k, bgs.k, out=bgs[:], in0=bgu[:, :, 0:8], scalar1=1.702, scalar2=None, op0=ALU.mult)
            P.V("tensor_scalar", bgu.k, bgu1.k, out=bgu1[:], in0=bgu[:, :, 8:16], scalar1=1.0, scalar2=None, op0=ALU.add)
            hq = Buf(P, "hq", [128, 8, 1024], BF16)
            acc = Buf(P, "acc", [128, 8, D], F32, n=8)
            comb = Buf(P, "comb", [128, 8, NE], F32, n=8)
            for qd in range(4):
                t0 = qd * 1024
                P.dma(SP, [], hq.k, hq[:], h1T_d[:, :, t0:t0 + 1024])
                with contextlib.ExitStack() as es1:
                    P.es = es1
                    wpg = Buf(P, "wpg", [128, 8, D], BF16)
                    wpp = Buf(P, "wpp", [128, 2, D], BF16)
                    P.dma(POOL, [], wpg.k, wpg[:], w_ple_gate[l].rearrange("(kc k) n -> k kc n", k=128))
                    P.dma(POOL, [], wpp.k, wpp[:], w_ple_proj[l].rearrange("(kc k) n -> k kc n", k=128))
                    h1t = [Buf(P, "h1t%d" % i, [128, D], F32) for i in range(2)]
                    hTf = [Buf(P, "hTf%d" % i, [128, 8, 128], F32) for i in range(2)]
                    pb = [Buf(P, "pb%d" % i, [128, 256], BF16) for i in range(2)]
                    pT = [Buf(P, "pT%d" % i, [128, 2, 128], BF16) for i in range(2)]
                    lg = [Buf(P, "lg%d" % i, [128, NE], F32) for i in range(2)]
                    ex = [Buf(P, "ex%d" % i, [128, NE], F32) for i in range(2)]
                    m8 = [Buf(P, "m8r%d" % i, [128, 16], F32) for i in range(2)]
                    cT = [Buf(P, "cT%d" % i, [NE, 128], F32) for i in range(2)]
                    sgp = [Buf(P, "sgp%d" % i, [128, D], F32) for i in range(2)]
                    pstf = [Buf(P, "pstf%d" % i, [128, 4, 128], F32, psum=True) for i in range(2)]
                    pstb = Buf(P, "pstb", [128, 8, 128], BF16, psum=True)
                    psl = Buf(P, "psl", [128, 512], F32, psum=True)
                    psb = Buf(P, "psb", [128, 2, 512], F32, psum=True)
                    pspg = Buf(P, "pspg", [128, 2, 512], F32, psum=True)
                    for t8 in range(8):
                        tt = qd * 8 + t8
                        i2 = t8 % 2
                        hh, hf_, lg_, ex_, m_ = h1t[i2], hTf[i2], lg[i2], ex[i2], m8[i2]
                        tl = slice(t8 * 128, (t8 + 1) * 128)
                        P.dma(SP, [], hh.k, hh[:], h1_tok[tt * 128:(tt + 1) * 128, :])
                        for half in range(2):
                            pt = pstf[half]
                            for c4 in range(4):
                                kc = half * 4 + c4
                                P.tr(hh.k + identf.k, pt.k, pt[:, c4, :], hh[:, kc * 128:(kc + 1) * 128], identf[:])
                            evac(pt.k, hf_.k, hf_[:, half * 4:half * 4 + 4, :], pt[:, :, :])
                        for kc in range(8):
                            P.mm(hf_.k + wr.k, psl.k, psl[:, 0:NE], hf_[:, kc, :], wr[:, kc, :], start=(kc == 0), stop=(kc == 7))
                        P.V("tensor_tensor", psl.k + brt.k, lg_.k, out=lg_[:], in0=psl[:, 0:NE], in1=brt[:], op=ALU.add)
                        P.V("max", lg_.k, m_.k, out=m_[:, 0:8], in_=lg_[:])
                        P.V("tensor_scalar", m_.k, m_.k, out=m_[:, 8:9], in0=m_[:, 0:1], scalar1=-1.0, scalar2=None, op0=ALU.mult)
                        P.A(lg_.k + m_.k, ex_.k, ex_[:], lg_[:], AF.Exp, bias=m_[:, 8:9])
                        P.V("scalar_tensor_tensor", lg_.k + m_.k + ex_.k, ex_.k, out=ex_[:], in0=lg_[:], scalar=m_[:, 3:4],
                            in1=ex_[:], op0=ALU.is_ge, op1=ALU.mult)
                        P.V("tensor_reduce", ex_.k, m_.k, out=m_[:, 9:10], in_=ex_[:], axis=AX.X, op=ALU.add)
                        P.V("reciprocal", m_.k, m_.k, out=m_[:, 10:11], in_=m_[:, 9:10])
                        P.V("tensor_scalar", ex_.k + m_.k, [comb.k[t8]], out=comb[:, t8, :], in0=ex_[:], scalar1=m_[:, 10:11],
                            scalar2=None, op0=ALU.mult)
                        pt = pstf[0]
                        P.tr([comb.k[t8]] + identf.k, pt.k, pt[0:NE, 0, :], comb[:, t8, :], identf[:])
                        c_ = cT[i2]
                        evac(pt.k, c_.k, c_[:], pt[0:NE, 0, :])
                        for hf in range(2):
                            P.mm(c_.k + bd.k, psb.k, psb[:, hf, :], c_[:], bd[:, hf * 512:(hf + 1) * 512])
                        pb_, pT_ = pb[i2], pT[i2]
                        P.dma(POOL, [], pb_.k, pb_[:], p[l, tt * 128:(tt + 1) * 128, :])
                        for c2 in range(2):
                            P.tr(pb_.k + identb.k, pstb.k, pstb[:, c2, :], pb_[:, c2 * 128:(c2 + 1) * 128], identb[:])
                        evac(pstb.k, pT_.k, pT_[:, :, :], pstb[:, 0:2, :])
                        for hf in range(2):
                            for kc in range(8):
                                P.mm(hq.k + wpg.k, pspg.k, pspg[:, hf, :], hq[:, kc, tl], wpg[:, kc, hf * 512:(hf + 1) * 512],
                                     start=(kc == 0), stop=(kc == 7))
                        sp_ = sgp[i2]
                        P.A(pspg.k, sp_.k, sp_[:, :].rearrange("t (a b) -> t a b", a=2), pspg[:, :, :], AF.Sigmoid)
                        for hf in range(2):
                            for kc in range(2):
                                P.mm(pT_.k + wpp.k, pspg.k, pspg[:, hf, :], pT_[:, kc, :], wpp[:, kc, hf * 512:(hf + 1) * 512],
                                     start=(kc == 0), stop=(kc == 1))
                        P.V("tensor_tensor", sp_.k + pspg.k, sp_.k, out=sp_[:, :].rearrange("t (a b) -> t a b", a=2),
                            in0=sp_[:, :].rearrange("t (a b) -> t a b", a=2), in1=pspg[:, :, :], op=ALU.mult)
                        ak = [acc.k[t8]]
                        av = acc[:, t8, :].rearrange("t (a b) -> t a b", a=2)
                        P.V("scalar_tensor_tensor", hh.k + psb.k, ak, out=av, in0=hh[:, :].rearrange("t (a b) -> t a b", a=2),
                            scalar=ALPHA, in1=psb[:, :, :], op0=ALU.mult, op1=ALU.add)
                        P.G("tensor_tensor", ak + sp_.k, ak, out=acc[:, t8, :], in0=acc[:, t8, :], in1=sp_[:], op=ALU.add)
                    P.barrier()
                P.es = es
                es2 = contextlib.ExitStack()
                P.es = es2
                wgu = [Buf(P, "wgu%d" % i, [128, 8, 2 * DFF], BF16) for i in range(2)]
                wdn = [Buf(P, "wdn%d" % i, [128, 8, D], BF16) for i in range(2)]
                actT = [Buf(P, "actT%d" % i, [128, 8, 512], BF16) for i in range(2)]
                sgb = [Buf(P, "sgb%d" % i, [128, 512], BF16) for i in range(2)]
                gmb = [Buf(P, "gmb%d" % i, [128, 512], BF16) for i in range(2)]
                umb = [Buf(P, "umb%d" % i, [128, 512], BF16) for i in range(2)]
                psg = [Buf(P, "psg%d" % i, [128, 512], F32, psum=True) for i in range(2)]
                psu = [Buf(P, "psu%d" % i, [128, 512], F32, psum=True) for i in range(2)]
                psd = [Buf(P, "psd%d" % i, [128, 2, 512], F32, psum=True) for i in range(2)]

                wst = [Buf(P, "wst%d" % i, [128, 1024], F32) for i in range(3)]
                NPC = 24

                def piece(e, n):
                    kc, pc = divmod(n, 3)
                    if pc < 2:
                        return w_gu[l, e, kc * 128:(kc + 1) * 128, pc * 1024:(pc + 1) * 1024]
                    return w_down[l, e, kc * 128:(kc + 1) * 128, :]

                def expert_loader(e, slot):
                    for n in range(min(2, NPC)):
                        P.dma(SP, [], wst[n % 3].k, wst[n % 3][:], piece(e, n))
                    for n in range(NPC):
                        if n + 2 < NPC:
                            P.dma(SP, [], wst[(n + 2) % 3].k, wst[(n + 2) % 3][:], piece(e, n + 2))
                        kc, pc = divmod(n, 3)
                        sb_ = wst[n % 3]
                        if pc < 2:
                            P.A(sb_.k, wgu[slot].k, wgu[slot][:, kc, pc * 1024:(pc + 1) * 1024], sb_[:], AF.Copy)
                        else:
                            P.A(sb_.k, wdn[slot].k, wdn[slot][:, kc, :], sb_[:], AF.Copy)
                        yield

                for _ in expert_loader(0, 0):
                    pass
                it = 0
                for e in range(NE):
                    slot = e % 2
                    ldr = expert_loader(e + 1, (e + 1) % 2) if e + 1 < NE else iter(())
                    wg_, wd_ = wgu[slot], wdn[slot]
                    for tc in range(2):
                        tsl = slice(tc * 512, (tc + 1) * 512)
                        at = actT[tc % 2]
                        for j in range(8):
                            i2 = it % 2; it += 1
                            pg, pu = psg[i2], psu[i2]
                            next(ldr, None)
                            next(ldr, None)
                            for kc in range(8):
                                P.mm(wg_.k + hq.k, pg.k, pg[:, :], wg_[:, kc, j * 128:(j + 1) * 128], hq[:, kc, tsl],
                                     start=(kc == 0), stop=(kc == 7))
                            for kc in range(8):
                                P.mm(wg_.k + hq.k, pu.k, pu[:, :], wg_[:, kc, DFF + j * 128:DFF + (j + 1) * 128], hq[:, kc, tsl],
                                     start=(kc == 0), stop=(kc == 7))
                            s_, g_, u_ = sgb[i2], gmb[i2], umb[i2]
                            P.A(pg.k + bgs.k, s_.k, s_[:], pg[:, :], AF.Sigmoid, scale=1.702, bias=bgs[:, e, j:j + 1])
                            P.V("tensor_scalar", pg.k + bgu.k, g_.k, out=g_[:], in0=pg[:, :], scalar1=bgu[:, e, j:j + 1],
                                scalar2=7.0, op0=ALU.add, op1=ALU.min)
                            P.V("tensor_scalar", pu.k + bgu1.k, u_.k, out=u_[:], in0=pu[:, :], scalar1=bgu1[:, e, j:j + 1],
                                scalar2=8.0, op0=ALU.add, op1=ALU.min)
                            P.V("scalar_tensor_tensor", s_.k + g_.k, g_.k, out=g_[:], in0=s_[:], scalar=SIG_MAX, in1=g_[:],
                                op0=ALU.min, op1=ALU.mult)
                            P.V("scalar_tensor_tensor", g_.k + u_.k, at.k, out=at[:, j, :], in0=u_[:], scalar=-6.0, in1=g_[:],
                                op0=ALU.max, op1=ALU.mult)
                        for t4 in range(4):
                            t8 = tc * 4 + t4
                            pd = psd[t4 % 2]
                            for hf in range(2):
                                for j in range(8):
                                    P.mm(at.k + wd_.k, pd.k, pd[:, hf, :], at[:, j, t4 * 128:(t4 + 1) * 128],
                                         wd_[:, j, hf * 512:(hf + 1) * 512], start=(j == 0), stop=(j == 7))
                            ak = [acc.k[t8]]
                            av = acc[:, t8, :].rearrange("t (a b) -> t a b", a=2)
                            P.V("scalar_tensor_tensor", pd.k + ak + [comb.k[t8]], ak, out=av, in0=pd[:, :, :],
                                scalar=comb[:, t8, e:e + 1], in1=av, op0=ALU.mult, op1=ALU.add)
                    for _ in ldr:
                        pass
                P.barrier()
                es2.close()
                P.es = es
                with contextlib.ExitStack() as es3:
                    P.es = es3
                    st6 = [Buf(P, "st6D%d" % i, [128, 12], F32) for i in range(2)]
                    mv = [Buf(P, "mvD%d" % i, [128, 4], F32) for i in range(2)]
                    h2 = [Buf(P, "h2_%d" % i, [128, D], F32) for i in range(2)]
                    h2b = [Buf(P, "h2b%d" % i, [128, D], BF16) for i in range(2)]
                    stg = [Buf(P, "stgD%d" % i, [128, 8, 512], BF16) for i in range(2)]
                    pst = [Buf(P, "psDt%d" % i, [128, 8, 128], BF16, psum=True) for i in range(2)]
                    gt = Buf(P, "g2", [128, D], F32)
                    bt = Buf(P, "b2", [128, D], F32)
                    P.dma(SP, [], gt.k, gt[:], ln2_g[l:l + 1, :].to_broadcast([128, D]))
                    P.dma(SP, [], bt.k, bt[:], ln2_b[l:l + 1, :].to_broadcast([128, D]))
                    for t8 in range(8):
                        tt = qd * 8 + t8
                        i2 = t8 % 2
                        a2 = acc[:, t8, :]
                        ak = [acc.k[t8]]
                        s6, m_ = st6[i2], mv[i2]
                        for hf in range(2):
                            P.V("bn_stats", ak, s6.k, out=s6[:, hf * 6:(hf + 1) * 6], in_=acc[:, t8, hf * 512:(hf + 1) * 512])
                        P.V("bn_aggr", s6.k, m_.k, out=m_[:, 0:2], in_=s6[:, :])
                        P.V("tensor_scalar", m_.k, m_.k, out=m_[:, 2:3], in0=m_[:, 1:2], scalar1=EPS, scalar2=None, op0=ALU.add)
                        P.A(m_.k, m_.k, m_[:, 2:3], m_[:, 2:3], AF.Sqrt)
                        P.V("reciprocal", m_.k, m_.k, out=m_[:, 3:4], in_=m_[:, 2:3])
                        P.V("tensor_scalar", ak + m_.k, ak, out=a2, in0=a2, scalar1=m_[:, 0:1], scalar2=m_[:, 3:4],
                            op0=ALU.subtract, op1=ALU.mult)
                        P.G("tensor_tensor", ak + gt.k, ak, out=a2, in0=a2, in1=gt[:], op=ALU.mult)
                        h2_ = h2[i2]
                        P.G("tensor_tensor", ak + bt.k, h2_.k, out=h2_[:], in0=a2, in1=bt[:], op=ALU.add)
                        P.dma(SP, h2_.k, [], out_tok[tt * 128:(tt + 1) * 128, :], h2_[:])
                        if outT_d is not None:
                            hb = h2b[i2]
                            P.A(h2_.k, hb.k, hb[:], h2_[:], AF.Copy)
                            pt = pst[i2]
                            for kc in range(8):
                                P.tr(hb.k + identb.k, pt.k, pt[:, kc, :], hb[:, kc * 128:(kc + 1) * 128], identb[:])
                            sg_ = stg[(t8 // 4) % 2]
                            evac(pt.k, sg_.k, sg_[:, :, (t8 % 4) * 128:(t8 % 4 + 1) * 128], pt[:, :, :])
                            if t8 % 4 == 3:
                                c0 = qd * 1024 + (t8 // 4) * 512
                                P.dma(SP, sg_.k, [], outT_d[:, :, c0:c0 + 512], sg_[:])
                    P.barrier()
                P.es = es
            P.barrier()
        P.es = top

    with contextlib.ExitStack() as es0:
        P.es = es0
        tok_to_T(es0, x, hT_d, "p0")
        P.barrier()
    P.es = top
    for l in range(n_layers):
        res_tok = x if l == 0 else h_tok
        last = (l == n_layers - 1)
        with contextlib.ExitStack() as esl:
            P.es = esl
            XT = Buf(P, "XT", [128, 8, S], BF16)
            P.dma(SP, [], XT.k, XT[:, 0:4, :], hT_d[:, 0:4, :])
            P.dma(SP, [], XT.k, XT[:, 4:8, :], hT_d[:, 4:8, :])
            if "A" in phases:
                phase_sb(l, XT)
            if "B" in phases:
                phase_nsa(l, XT)
                P.dead = False
            P.barrier()
        P.es = top
        if "C" in phases:
            phase_merge(l, res_tok)
        if "D" in phases:
            phase_moe(l, y if last else h_tok, None if last else hT_d)
    P.barrier()
    top.close()
    return nc, P


def _bf(a):
    return np.asarray(a, dtype=np.float32).astype(ml_dtypes.bfloat16)


def make_consts():
    i = np.arange(128)
    c = {}
    c["c_identb"] = _bf(np.eye(128))
    c["c_identf"] = np.eye(128, dtype=np.float32)
    c["c_triui"] = _bf(i[:, None] >= i[None, :])
    c["c_ones"] = _bf(np.ones((128, 128)))
    ii = np.arange(512)
    c["c_masksb"] = _bf(ii[None, None, :] > (128 * np.arange(4)[None, :, None] + i[:, None, None]))
    c["c_ident4"] = _bf(np.repeat(np.eye(128)[:, None, :], 4, axis=1))
    c["c_negC"] = _bf(np.where(i[None, :] > i[:, None], NEG, 0.0))
    c["c_negL"] = _bf(np.where(i[None, :] <= i[:, None], NEG, 0.0))
    tt = np.arange(NT)
    c["c_maskcmp"] = _bf((128 * tt[None, :, None] + i[None, None, :]) >= (32 * i[:, None, None] + 31))
    c["c_pairsum"] = _bf((i[:, None] // 2) == np.arange(64)[None, :])
    pos = 128 * tt[None, :, None] + i[:, None, None]
    cur = pos // 64
    b = np.arange(64)[None, None, :]
    forced = (b == 0) | (b == cur) | (b == cur - 1)
    valid = (b * 64) <= pos
    c["c_F"] = _bf(np.where(forced, 1.0e6, 0.0))
    c["c_V"] = _bf(np.where(valid, 0.0, -1.0e30))
    slopes = 2.0 ** (-8.0 * np.arange(1, 9) / 8.0)
    t = np.arange(S)
    qa = np.zeros((4, 8, S), np.float32)
    for h in range(8):
        qa[0, h, :] = 128.0 * slopes[h]
        qa[1, h, :] = slopes[h]
        qa[2, h, :] = -128.0 * slopes[h] * (t // 128)
        qa[3, h, :] = -slopes[h] * (t % 128)
    c["c_qaug"] = _bf(qa)
    ka = np.stack([t // 128, t % 128, np.ones(S), np.ones(S)]).astype(np.float32)
    c["c_kaug"] = _bf(ka)
    ce = 32 * np.arange(128) + 31
    c["c_kcaug"] = _bf(np.stack([ce // 128, ce % 128, np.ones(128), np.ones(128)]).astype(np.float32))
    return c


_CACHE = {}


def make_in_maps(x, p, w_in, w_cmp1, w_cmp2, pe_cmp, w_br_sb, w_br_nsa, w_o, ln1_g, ln1_b, ln2_g, ln2_b,
                 w_router, b_router, w_gu, b_gu, w_down, b_down, w_ple_gate, w_ple_proj, cores=range(8)):
    f = lambda a: np.ascontiguousarray(np.asarray(a, dtype=np.float32))
    shared = dict(
        w_in=f(w_in), w_cmp1=f(w_cmp1), w_cmp2=f(w_cmp2),
        pe_cmpT=f(np.transpose(np.asarray(pe_cmp), (0, 1, 3, 2))),
        w_br_sb=f(w_br_sb), w_br_nsa=f(w_br_nsa), w_o=f(w_o), ln1_g=f(ln1_g), ln1_b=f(ln1_b),
        ln2_g=f(ln2_g), ln2_b=f(ln2_b), w_router=f(w_router), b_router=f(b_router), w_gu=f(w_gu),
        b_guT=f(np.transpose(np.asarray(b_gu).reshape(DEPTH, NE, 16, 128), (0, 3, 1, 2))),
        w_down=f(w_down), b_down=f(b_down), w_ple_gate=f(w_ple_gate), w_ple_proj=f(w_ple_proj),
    )
    shared.update(make_consts())
    x = np.asarray(x, dtype=np.float32)
    p = np.asarray(p, dtype=np.float32)
    in_maps = []
    for b in cores:
        m = dict(shared)
        m["x"] = np.ascontiguousarray(x[b])
        m["p"] = np.ascontiguousarray(p[:, b])
        in_maps.append(m)
    return in_maps


def kernel(**inputs):
    if "nc" not in _CACHE:
        _CACHE["nc"] = build()[0]
    nc = _CACHE["nc"]
    in_maps = make_in_maps(**inputs)
    res = run_bass_kernel_spmd(nc, in_maps, core_ids=list(range(8)))
    return np.stack([np.asarray(r["y"], dtype=np.float32) for r in res.results], axis=0)
```

```python
import contextlib
import numpy as np
import ml_dtypes
import concourse.bass as bass
import concourse.mybir as mybir
from concourse.bass_utils import run_bass_kernel_spmd

F32 = mybir.dt.float32
BF16 = mybir.dt.bfloat16
ALU = mybir.AluOpType
AF = mybir.ActivationFunctionType
AX = mybir.AxisListType

S = 4096
D = 1024
NT = S // 128
DEPTH = 4
N_IN = 4888
NE = 32
DFF = 1024
ALPHA = (2 * DEPTH) ** 0.25
EPS = 1e-5
NEG = -30000.0
O_SBQ, O_SBK, O_SBV, O_NQ = 0, 512, 1024, 1536
O_KC, O_VC, O_KS, O_VS, O_KW, O_VW = 2048, 2176, 2304, 2432, 2560, 2688
O_GATE, O_GSB, O_GNSA = 2816, 2840, 3864
SIG_MAX = float(1.0 / (1.0 + np.exp(-1.702 * 7.0)))


class Src:
    __slots__ = ("sem", "val")

    def __init__(self, sem):
        self.sem = sem
        self.val = 0


class Trk:
    __slots__ = ("w", "r", "x")

    def __init__(self, x=False):
        self.w = None
        self.r = {}
        self.x = x


class Eng:
    def __init__(self, name, h, sem):
        self.name = name
        self.h = h
        self.src = Src(sem)
        self.known = {}
        self.slots = []
        self.slot_i = 0


class Prog:
    def __init__(self, nc, es):
        self.nc = nc
        self.es = es
        mk = lambda n, h: Eng(n, h, es.enter_context(nc.semaphore("s_" + n)))
        self.pe = mk("pe", nc.tensor)
        self.act = mk("act", nc.scalar)
        self.dve = mk("dve", nc.vector)
        self.pool = mk("pool", nc.gpsimd)
        self.sp = mk("sp", nc.sync)
        self.engs = [self.pe, self.act, self.dve, self.pool, self.sp]
        for e, n in ((self.sp, 20), (self.pool, 20), (self.act, 6)):
            e.slots = [Src(es.enter_context(nc.semaphore("d_%s%d" % (e.name, i)))) for i in range(n)]
        self.ninst = 0
        self.dead = False

    def _deps(self, E, R, W):
        deps = {}
        for t in R:
            if t.w is not None:
                s, v = t.w
                if deps.get(s, 0) < v:
                    deps[s] = v
        for t in W:
            if t.w is not None:
                s, v = t.w
                if deps.get(s, 0) < v:
                    deps[s] = v
            for s, v in t.r.items():
                if deps.get(s, 0) < v:
                    deps[s] = v
        for s, v in deps.items():
            if s is E.src and E is self.pe:
                continue
            if E.known.get(s, 0) < v:
                E.h.wait_ge(s.sem, v)
                E.known[s] = v
                self.ninst += 1

    def op(self, E, fn, R, W, *a, **kw):
        if self.dead:
            return None
        W = list(W) + [t for t in R if t.x]
        R = [t for t in R if not t.x]
        self._deps(E, R, W)
        ins = getattr(E.h, fn)(*a, **kw)
        E.src.val += 1
        ins.then_inc(E.src.sem, 1)
        self.ninst += 1
        v = E.src.val
        for t in R:
            t.r[E.src] = v
        for t in W:
            t.w = (E.src, v)
            t.r = {}
        return ins

    def dma(self, Q, R, W, out, in_):
        if self.dead:
            return None
        self._deps(Q, R, W)
        sl = Q.slots[Q.slot_i]
        Q.slot_i = (Q.slot_i + 1) % len(Q.slots)
        if Q.known.get(sl, 0) < sl.val:
            Q.h.wait_ge(sl.sem, sl.val)
            Q.known[sl] = sl.val
        ins = Q.h.dma_start(out=out, in_=in_)
        sl.val += 16
        ins.then_inc(sl.sem, 16)
        self.ninst += 1
        for t in R:
            t.r[sl] = sl.val
        for t in W:
            t.w = (sl, sl.val)
            t.r = {}

    def barrier(self):
        srcs = [e.src for e in self.engs]
        for e in self.engs:
            srcs += e.slots
        for E in self.engs:
            for s in srcs:
                if s is E.src or s.val == 0:
                    continue
                if E.known.get(s, 0) < s.val:
                    E.h.wait_ge(s.sem, s.val)
                    E.known[s] = s.val

    def mm(self, R, W, out, lhsT, rhs, start=True, stop=True, sgc=False):
        if sgc:
            return self.op(self.pe, "matmul", R, W, out, lhsT=lhsT, rhs=rhs, start=start, stop=stop,
                           skip_group_check=True)
        return self.op(self.pe, "matmul", R, W, out, lhsT=lhsT, rhs=rhs, start=start, stop=stop)

    def tr(self, R, W, out, in_, ident):
        return self.op(self.pe, "transpose", R, W, out, in_, ident)

    def A(self, R, W, out, in_, func, **kw):
        return self.op(self.act, "activation", R, W, out=out, in_=in_, func=func, **kw)

    def V(self, fn, R, W, **kw):
        return self.op(self.dve, fn, R, W, **kw)

    def G(self, fn, R, W, **kw):
        return self.op(self.pool, fn, R, W, **kw)


class Buf:
    _n = [0]

    def __init__(self, P, name, shape, dtype, n=1, psum=False):
        nc = P.nc
        Buf._n[0] += 1
        name = "%s_%d" % (name, Buf._n[0])
        cm = nc.psum_tensor(name, shape, dtype) if psum else nc.sbuf_tensor(name, shape, dtype)
        self.t = P.es.enter_context(cm)
        self.k = [Trk(x=psum) for _ in range(n)]

    def __getitem__(self, key):
        return self.t[key]


class _Stop(Exception):
    pass


def build(n_layers=DEPTH, debug=False, phases="ABCD", bstop=99):
    nc = bass.Bass("TRN2", target_bir_lowering=False)
    dt_in = lambda name, shape, dt=F32: nc.dram_tensor(name, shape, dt, kind="ExternalInput").ap()
    x = dt_in("x", [S, D])
    p = dt_in("p", [DEPTH, S, 256])
    w_in = dt_in("w_in", [DEPTH, D, N_IN])
    w_cmp1 = dt_in("w_cmp1", [DEPTH, 2, 32, 64, 128])
    w_cmp2 = dt_in("w_cmp2", [DEPTH, 2, 128, 64])
    pe_cmpT = dt_in("pe_cmpT", [DEPTH, 2, 64, 32])
    w_br_sb = dt_in("w_br_sb", [DEPTH, 512, D])
    w_br_nsa = dt_in("w_br_nsa", [DEPTH, 512, D])
    w_o = dt_in("w_o", [DEPTH, D, D])
    ln1_g = dt_in("ln1_g", [DEPTH, D])
    ln1_b = dt_in("ln1_b", [DEPTH, D])
    ln2_g = dt_in("ln2_g", [DEPTH, D])
    ln2_b = dt_in("ln2_b", [DEPTH, D])
    w_router = dt_in("w_router", [DEPTH, D, NE])
    b_router = dt_in("b_router", [DEPTH, NE])
    w_gu = dt_in("w_gu", [DEPTH, NE, D, 2 * DFF])
    b_guT = dt_in("b_guT", [DEPTH, 128, NE, 16])
    w_down = dt_in("w_down", [DEPTH, NE, DFF, D])
    b_down = dt_in("b_down", [DEPTH, NE, D])
    w_ple_gate = dt_in("w_ple_gate", [DEPTH, D, D])
    w_ple_proj = dt_in("w_ple_proj", [DEPTH, 256, D])
    c_identb = dt_in("c_identb", [128, 128], BF16)
    c_identf = dt_in("c_identf", [128, 128], F32)
    c_triui = dt_in("c_triui", [128, 128], BF16)
    c_ones = dt_in("c_ones", [128, 128], BF16)
    c_masksb = dt_in("c_masksb", [128, 4, 512], BF16)
    c_ident4 = dt_in("c_ident4", [128, 4, 128], BF16)
    c_negC = dt_in("c_negC", [128, 128], BF16)
    c_negL = dt_in("c_negL", [128, 128], BF16)
    c_maskcmp = dt_in("c_maskcmp", [128, NT, 128], BF16)
    c_pairsum = dt_in("c_pairsum", [128, 64], BF16)
    c_F = dt_in("c_F", [128, NT, 64], BF16)
    c_V = dt_in("c_V", [128, NT, 64], BF16)
    c_qaug = dt_in("c_qaug", [4, 8, S], BF16)
    c_kaug = dt_in("c_kaug", [4, S], BF16)
    c_kcaug = dt_in("c_kcaug", [4, 128], BF16)

    y = nc.dram_tensor("y", [S, D], F32, kind="ExternalOutput").ap()
    skind = "ExternalOutput" if debug else "Internal"
    hT_d = nc.dram_tensor("hT_d", [128, 8, S], BF16, kind=skind).ap()
    h1T_d = nc.dram_tensor("h1T_d", [128, 8, S], BF16, kind=skind).ap()
    h_tok = nc.dram_tensor("h_tok", [S, D], F32, kind=skind).ap()
    h1_tok = nc.dram_tensor("h1_tok", [S, D], F32, kind=skind).ap()
    ysbT_d = nc.dram_tensor("ysbT_d", [128, 4, S], BF16, kind=skind).ap()
    ynsaT_d = nc.dram_tensor("ynsaT_d", [128, 4, S], BF16, kind=skind).ap()

    top = contextlib.ExitStack()
    P = Prog(nc, top)
    PE, ACT, DVE, POOL, SP = P.pe, P.act, P.dve, P.pool, P.sp

    identb = Buf(P, "identb", [128, 128], BF16)
    identf = Buf(P, "identf", [128, 128], F32)
    P.dma(SP, [], identb.k, identb[:], c_identb)
    P.dma(SP, [], identf.k, identf[:], c_identf)

    rr = [0]

    def chk(n):
        if n >= bstop:
            P.dead = True

    def evac(R, W, out, in_, scale=None):
        rr[0] ^= 1
        if rr[0]:
            if scale is None:
                P.A(R, W, out, in_, AF.Copy)
            else:
                P.A(R, W, out, in_, AF.Copy, scale=scale)
        else:
            if scale is None:
                P.V("tensor_copy", R, W, out=out, in_=in_)
            else:
                P.V("tensor_scalar", R, W, out=out, in0=in_, scalar1=scale, scalar2=None, op0=ALU.mult)

    def win_cols(l, c0, w):
        return w_in[l].rearrange("(kc k) n -> k kc n", k=128)[:, :, c0:c0 + w]

    def tok_to_T(es, src_tok, dstT_d, name):
        xt = [Buf(P, name + "xt%d" % i, [128, D], BF16) for i in range(2)]
        st = [Buf(P, name + "st%d" % i, [128, 8, 512], BF16) for i in range(2)]
        pst = [Buf(P, name + "ps%d" % i, [128, 8, 128], BF16, psum=True) for i in range(2)]
        for tt in range(NT):
            b = xt[tt % 2]
            P.dma(POOL, [], b.k, b[:], src_tok[tt * 128:(tt + 1) * 128, :])
            ps = pst[tt % 2]
            for kc in range(8):
                P.tr(b.k + identb.k, ps.k, ps[:, kc, :], b[:, kc * 128:(kc + 1) * 128], identb[:])
            sb = st[(tt // 4) % 2]
            evac(ps.k, sb.k, sb[:, :, (tt % 4) * 128:(tt % 4 + 1) * 128], ps[:, :, :])
            if tt % 4 == 3:
                ch = tt // 4
                P.dma(SP, sb.k, [], dstT_d[:, :, ch * 512:(ch + 1) * 512], sb[:])

    def phase_sb(l, XT):
        with contextlib.ExitStack() as es:
            P.es = es
            triui = Buf(P, "triui", [128, 128], BF16)
            ones = Buf(P, "ones", [128, 128], BF16)
            masksb = Buf(P, "masksb", [128, 4, 512], BF16)
            P.dma(SP, [], triui.k, triui[:], c_triui)
            P.dma(SP, [], ones.k, ones[:], c_ones)
            P.dma(SP, [], masksb.k, masksb[:], c_masksb)
            W3 = [Buf(P, "W3_%d" % i, [128, 8, 384], BF16) for i in range(2)]
            qT = [Buf(P, "qT%d" % i, [128, S], BF16) for i in range(2)]
            kT = [Buf(P, "kT%d" % i, [128, S], BF16) for i in range(2)]
            kN = [Buf(P, "kN%d" % i, [128, S], BF16) for i in range(2)]
            Vt = [Buf(P, "Vt%d" % i, [128, NT, 128], BF16) for i in range(2)]
            yT = [Buf(P, "yT%d" % i, [128, S], BF16) for i in range(2)]
            NB = 6
            psz = [Buf(P, "psz%d" % i, [128, 512], F32, psum=True) for i in range(NB)]
            pso = [Buf(P, "pso%d" % i, [128, 512], F32, psum=True) for i in range(2)]
            psp = psz[0:2]
            ez = [Buf(P, "ez%d" % i, [128, 512], F32) for i in range(NB)]
            spb = [Buf(P, "spb%d" % i, [128, 512], BF16) for i in range(NB)]
            wTb = [Buf(P, "wTb%d" % i, [128, 512], BF16) for i in range(NB)]
            csb = [Buf(P, "csb%d" % i, [128, 512], BF16) for i in range(2)]
            for hp in range(4):
                w3 = W3[hp % 2]
                for j, c0 in enumerate((O_SBQ, O_SBK, O_SBV)):
                    P.dma(POOL, [], w3.k, w3[:, :, j * 128:(j + 1) * 128], win_cols(l, c0 + hp * 128, 128))
                q, k, kn, v, yt = qT[hp % 2], kT[hp % 2], kN[hp % 2], Vt[hp % 2], yT[hp % 2]
                for j in range(2):
                    for ch in range(8):
                        ps = psp[ch % 2]
                        csl = slice(ch * 512, (ch + 1) * 512)
                        for kc in range(8):
                            P.mm(w3.k + XT.k, ps.k, ps[:, :], w3[:, kc, j * 128:(j + 1) * 128],
                                 XT[:, kc, csl], start=(kc == 0), stop=(kc == 7))
                        if j == 0:
                            P.A(ps.k, q.k, q[:, csl], ps[:, :], AF.Copy, scale=0.125)
                        else:
                            P.V("tensor_scalar", ps.k, kn.k, out=kn[:, csl], in0=ps[:, :], scalar1=-1.0, scalar2=None,
                                op0=ALU.mult)
                for tt in range(NT):
                    ps = psp[tt % 2]
                    for kc in range(8):
                        P.mm(w3.k + XT.k, ps.k, ps[:, 0:128], XT[:, kc, tt * 128:(tt + 1) * 128],
                             w3[:, kc, 256:384], start=(kc == 0), stop=(kc == 7))
                    evac(ps.k, v.k, v[:, tt, :], ps[:, 0:128])
                items = []
                for hh in range(2):
                    for qt in range(8):
                        kbs = list(range(4 * qt + 3, -1, -1))
                        for n, kb in enumerate(kbs):
                            items.append((hh, qt, kb, n == 0, n == len(kbs) - 1))
                cs_i = [0]

                def stage1(i):
                    hh, qt, kb, first, last = items[i]
                    b0 = 64 * hh
                    ib = i % NB
                    pz, e_, s_ = psz[ib], ez[ib], spb[ib]
                    P.mm(kn.k + q.k, pz.k, pz[:, :], kn[b0:b0 + 64, kb * 128:(kb + 1) * 128],
                         q[b0:b0 + 64, qt * 512:(qt + 1) * 512], start=True, stop=False, sgc=True)
                    P.A(pz.k, e_.k, e_[:], pz[:, :], AF.Exp, scale=-1.0)
                    P.A(e_.k, s_.k, s_[:], e_[:], AF.Ln, bias=1.0)
                    if kb >= 4 * qt:
                        P.V("tensor_tensor", s_.k + masksb.k, s_.k, out=s_[:], in0=s_[:],
                            in1=masksb[:, kb - 4 * qt, :], op=ALU.mult)

                def stage2a(i):
                    hh, qt, kb, first, last = items[i]
                    ib = i % NB
                    pc, s_, w_ = psz[ib], spb[ib], wTb[ib]
                    P.mm(triui.k + s_.k, pc.k, pc[:, :], triui[:], s_[:], start=False, stop=first, sgc=True)
                    if not first:
                        cb = csb[cs_i[0] % 2]
                        P.mm(ones.k + cb.k, pc.k, pc[:, :], ones[:], cb[:], start=False, stop=True, sgc=True)
                    if not last:
                        if first:
                            cs_i[0] += 1
                            cb = csb[cs_i[0] % 2]
                            P.V("tensor_copy", s_.k, cb.k, out=cb[:], in_=s_[:])
                        else:
                            cb0 = csb[cs_i[0] % 2]
                            cs_i[0] += 1
                            cb = csb[cs_i[0] % 2]
                            P.V("tensor_tensor", s_.k + cb0.k, cb.k, out=cb[:], in0=cb0[:], in1=s_[:], op=ALU.add)
                    P.A(pc.k, w_.k, w_[:], pc[:, :], AF.Exp, scale=-1.0)
                    if kb >= 4 * qt:
                        P.V("tensor_tensor", w_.k + masksb.k, w_.k, out=w_[:], in0=w_[:],
                            in1=masksb[:, kb - 4 * qt, :], op=ALU.mult)

                def stage2b(i):
                    hh, qt, kb, first, last = items[i]
                    b0 = 64 * hh
                    w_ = wTb[i % NB]
                    po = pso[(hh * 8 + qt) % 2]
                    P.mm(v.k + w_.k, po.k, po[b0:b0 + 64, :], v[:, kb, b0:b0 + 64], w_[:], start=first, stop=last)
                    if last:
                        evac(po.k, yt.k, yt[b0:b0 + 64, qt * 512:(qt + 1) * 512], po[b0:b0 + 64, :])

                LOOK = NB - 1
                n_it = len(items)
                for i in range(min(LOOK, n_it)):
                    stage1(i)
                stage2a(0)
                for i in range(n_it):
                    if i + LOOK < n_it:
                        stage1(i + LOOK)
                    if i + 1 < n_it:
                        stage2a(i + 1)
                    stage2b(i)
                P.dma(SP, yt.k, [], ysbT_d[:, hp, :], yt[:])
            P.barrier()
        P.es = top

    def phase_nsa(l, XT):
        with contextlib.ExitStack() as es:
            P.es = es
            ident4 = Buf(P, "ident4", [128, 4, 128], BF16)
            negC = Buf(P, "negC", [128, 128], BF16)
            negL = Buf(P, "negL", [128, 128], BF16)
            maskcmp = Buf(P, "maskcmp", [128, NT, 128], BF16)
            Ft = Buf(P, "Ft", [128, NT, 64], BF16)
            Vt_ = Buf(P, "Vt_", [128, NT, 64], BF16)
            for b, c in ((ident4, c_ident4), (negC, c_negC), (negL, c_negL), (maskcmp, c_maskcmp), (Ft, c_F), (Vt_, c_V)):
                P.dma(SP, [], b.k, b[:], c)
            Qa = Buf(P, "Qa", [68, 4, S], BF16)
            kwa = Buf(P, "kwa", [68, S], BF16)
            ksa = Buf(P, "ksa", [68, S], BF16)
            kca = Buf(P, "kca", [68, 128], BF16)
            vsa = Buf(P, "vsa", [128, NT, 68], BF16)
            vwa = Buf(P, "vwa", [128, NT, 68], BF16)
            rhsc = Buf(P, "rhsc", [128, 128], BF16)
            gates = Buf(P, "gates", [128, NT, 24], F32)
            Wg = Buf(P, "Wg", [128, 8, 24], BF16)
            NPS = 4
            pss = [Buf(P, "pss%d" % i, [128, 512], F32, psum=True) for i in range(NPS)]
            psoc = [Buf(P, "psoc%d" % i, [128, 4, 128], F32, psum=True) for i in range(1)]
            psos_b = Buf(P, "psos", [128, 512], F32, psum=True)
            psow_b = Buf(P, "psow", [128, 512], F32, psum=True)
            pst_b = Buf(P, "pstn", [128, 8, 128], BF16, psum=True)

            class _View:
                def __init__(self, b, ap):
                    self.k = b.k
                    self.ap = ap

                def __getitem__(self, key):
                    return self.ap[key]

            psos = _View(psos_b, psos_b[:, 0:260].rearrange("t (r d) -> t r d", r=4))
            psow = _View(psow_b, psow_b[:, 0:260].rearrange("t (r d) -> t r d", r=4))
            pst = _View(pst_b, pst_b[:, 0:2, :])
            psp = pss
            P.dma(SP, [], kwa.k, kwa[64:68, :], c_kaug)
            P.dma(SP, [], ksa.k, ksa[64:68, :], c_kaug)
            P.dma(SP, [], kca.k, kca[64:68, :], c_kcaug)
            P.dma(SP, [], rhsc.k, rhsc[:, 64:128], c_pairsum)
            P.G("memset", [], vsa.k, ap=vsa[:, :, 64:65], constant=1.0)
            P.G("memset", [], vwa.k, ap=vwa[:, :, 64:65], constant=1.0)
            P.dma(POOL, [], Wg.k, Wg[:], win_cols(l, O_GATE, 24))
            for tt in range(NT):
                ps = psp[tt % 2]
                for kc in range(8):
                    P.mm(Wg.k + XT.k, ps.k, ps[:, 0:24], XT[:, kc, tt * 128:(tt + 1) * 128], Wg[:, kc, :],
                         start=(kc == 0), stop=(kc == 7))
                P.A(ps.k, gates.k, gates[:, tt, :], ps[:, 0:24], AF.Sigmoid)
            it = 0
            chk(1)
            for g in range(2):
              with contextlib.ExitStack() as esp:
                P.es = esp
                kcl = [Buf(P, "kcl%d" % i, [64, 32, 128], BF16) for i in range(2)]
                w1 = [Buf(P, "w1_%d" % i, [64, 32, 128], BF16) for i in range(2)]
                w2 = [Buf(P, "w2_%d" % i, [128, 64], BF16) for i in range(2)]
                peT = [Buf(P, "peT%d" % i, [64, 32], F32) for i in range(2)]
                hid = [Buf(P, "hid%d" % i, [128, 128], BF16) for i in range(2)]
                sg = Buf(P, "sg", [128, 128], F32)
                Wq = Buf(P, "Wq", [128, 8, 256], BF16)
                Wk = Buf(P, "Wk", [128, 8, 256], BF16)
                Wv = Buf(P, "Wv", [128, 8, 128], BF16)
                P.dma(SP, [], Qa.k, Qa[64:68, :, :], c_qaug[:, 4 * g:4 * g + 4, :])
                P.dma(POOL, [], Wq.k, Wq[:], win_cols(l, O_NQ + 256 * g, 256))
                for j, c0 in enumerate((O_KC, O_VC, O_KS, O_KW)):
                    P.dma(POOL, [], Wk.k, Wk[:, :, j * 64:(j + 1) * 64], win_cols(l, c0 + 64 * g, 64))
                for j, c0 in enumerate((O_VS, O_VW)):
                    P.dma(POOL, [], Wv.k, Wv[:, :, j * 64:(j + 1) * 64], win_cols(l, c0 + 64 * g, 64))
                for j in range(2):
                    P.dma(POOL, [], w1[j].k, w1[j][:], w_cmp1[l, j].rearrange("l d h -> d l h"))
                    P.dma(POOL, [], w2[j].k, w2[j][:], w_cmp2[l, j])
                    P.dma(SP, [], peT[j].k, peT[j][:], pe_cmpT[l, j])
                for ch in range(8):
                    tsl = slice(ch * 512, (ch + 1) * 512)
                    for r in range(4):
                        ps = psp[it % NPS]; it += 1
                        for kc in range(8):
                            P.mm(Wq.k + XT.k, ps.k, ps[0:64, :], Wq[:, kc, r * 64:(r + 1) * 64], XT[:, kc, tsl],
                                 start=(kc == 0), stop=(kc == 7))
                        evac(ps.k, Qa.k, Qa[0:64, r, tsl], ps[0:64, :], scale=0.125)
                    for j in range(4):
                        ps = psp[it % NPS]; it += 1
                        for kc in range(8):
                            P.mm(Wk.k + XT.k, ps.k, ps[0:64, :], Wk[:, kc, j * 64:(j + 1) * 64], XT[:, kc, tsl],
                                 start=(kc == 0), stop=(kc == 7))
                        if j < 2:
                            dst = kcl[j]
                            evac(ps.k, dst.k, dst[:, :, ch * 16:(ch + 1) * 16],
                                 ps[0:64, :].rearrange("d (c l) -> d l c", l=32))
                        else:
                            dst = ksa if j == 2 else kwa
                            evac(ps.k, dst.k, dst[0:64, tsl], ps[0:64, :])
                chk(2)
                for tt in range(NT):
                    ps = psp[it % NPS]; it += 1
                    for kc in range(8):
                        P.mm(Wv.k + XT.k, ps.k, ps[:, 0:128], XT[:, kc, tt * 128:(tt + 1) * 128], Wv[:, kc, :],
                             start=(kc == 0), stop=(kc == 7))
                    evac(ps.k, vsa.k, vsa[:, tt, 0:64], ps[:, 0:64])
                    evac(ps.k, vwa.k, vwa[:, tt, 0:64], ps[:, 64:128])
                chk(3)
                for j in range(2):
                    P.V("tensor_tensor", kcl[j].k + peT[j].k, kcl[j].k, out=kcl[j][:], in0=kcl[j][:],
                        in1=peT[j][:, :].unsqueeze(2).to_broadcast([64, 32, 128]), op=ALU.add)
                    ps = psp[it % NPS]; it += 1
                    for li in range(32):
                        P.mm(w1[j].k + kcl[j].k, ps.k, ps[:, 0:128], w1[j][:, li, :], kcl[j][:, li, :],
                             start=(li == 0), stop=(li == 31))
                    P.A(ps.k, sg.k, sg[:], ps[:, 0:128], AF.Sigmoid)
                    P.V("tensor_tensor", sg.k + ps.k, hid[j].k, out=hid[j][:], in0=sg[:], in1=ps[:, 0:128], op=ALU.mult)
                    ps2 = psp[it % NPS]; it += 1
                    if j == 0:
                        P.mm(w2[j].k + hid[j].k, ps2.k, ps2[0:64, 0:128], w2[j][:], hid[j][:])
                        evac(ps2.k, kca.k, kca[0:64, :], ps2[0:64, 0:128])
                    else:
                        P.mm(w2[j].k + hid[j].k, ps2.k, ps2[:, 0:64], hid[j][:], w2[j][:])
                        evac(ps2.k, rhsc.k, rhsc[:, 0:64], ps2[:, 0:64])
                chk(4)
                P.barrier()
              with contextlib.ExitStack() as esm:
                P.es = esm
                sm = [dict((n, Buf(P, "%s%d" % (n, i), sh, F32)) for n, sh in
                           (("Z", [128, 12]), ("ri", [128, 12]), ("cf", [128, 12]), ("imp", [128, 64]),
                            ("m8", [128, 16]), ("imp2", [128, 64]), ("oc", [128, 4, 64]), ("t1", [128, 4, 64]),
                            ("t2", [128, 4, 64]))) for i in range(2)]
                nsel = [Buf(P, "nsel%d" % i, [128, 64], BF16) for i in range(2)]
                ytok = [Buf(P, "ytok%d" % i, [128, 256], BF16) for i in range(2)]
                yst = [Buf(P, "yst%d" % i, [128, 2, 512], BF16) for i in range(2)]
                nsx = [Buf(P, "nsx%d" % i, [128, S], BF16) for i in range(2)]
                eb = [Buf(P, "eb%d" % i, [128, 4, 128], BF16) for i in range(NPS)]
                def sel_stage(tt):
                    nonlocal it
                    s_ = sm[tt % 2]
                    tq = slice(tt * 128, (tt + 1) * 128)
                    Mc = 4 * (tt + 1)
                    pco = psoc[0]
                    ps = pss[it % NPS]; e_ = eb[it % NPS]; it += 1
                    P.mm(kca.k + Qa.k, ps.k, ps[0:Mc, :], kca[:, 0:Mc], Qa[:, :, tq])
                    P.A(ps.k, e_.k, e_[0:Mc, :, :], ps[0:Mc, :].rearrange("c (r t) -> c r t", r=4), AF.Exp)
                    P.V("tensor_tensor", e_.k + maskcmp.k, e_.k, out=e_[0:Mc, :, :], in0=e_[0:Mc, :, :],
                        in1=maskcmp[0:Mc, tt:tt + 1, :].to_broadcast([Mc, 4, 128]), op=ALU.mult)
                    for r in range(4):
                        P.mm(e_.k + rhsc.k, pco.k, pco[:, r, :], e_[0:Mc, r, :], rhsc[0:Mc, :])
                    Z, ri, cf = s_["Z"], s_["ri"], s_["cf"]
                    P.V("tensor_reduce", pco.k, Z.k, out=Z[:, 0:4], in_=pco[:, :, 64:128], axis=AX.X, op=ALU.add)
                    P.V("tensor_scalar", Z.k, Z.k, out=Z[:, 0:4], in0=Z[:, 0:4], scalar1=1e-30, scalar2=None, op0=ALU.max)
                    P.V("reciprocal", Z.k, ri.k, out=ri[:, 0:4], in_=Z[:, 0:4])
                    imp = s_["imp"]
                    P.V("tensor_scalar", pco.k + ri.k, imp.k, out=imp[:], in0=pco[:, 0, 64:128], scalar1=ri[:, 0:1],
                        scalar2=None, op0=ALU.mult)
                    for r in range(1, 4):
                        P.V("scalar_tensor_tensor", pco.k + ri.k + imp.k, imp.k, out=imp[:], in0=pco[:, r, 64:128],
                            scalar=ri[:, r:r + 1], in1=imp[:], op0=ALU.mult, op1=ALU.add)
                    oc = s_["oc"]
                    P.A(pco.k, oc.k, oc[:], pco[:, :, 0:64], AF.Copy)
                    P.V("tensor_tensor", imp.k + Ft.k, imp.k, out=imp[:], in0=imp[:], in1=Ft[:, tt, :], op=ALU.max)
                    P.V("tensor_tensor", imp.k + Vt_.k, imp.k, out=imp[:], in0=imp[:], in1=Vt_[:, tt, :], op=ALU.add)
                    m8, imp2 = s_["m8"], s_["imp2"]
                    P.V("max", imp.k, m8.k, out=m8[:, 0:8], in_=imp[:])
                    P.V("match_replace", m8.k + imp.k, imp2.k, out=imp2[:], in_to_replace=m8[:, 0:8], in_values=imp[:],
                        imm_value=-3.0e38)
                    P.V("max", imp2.k, m8.k, out=m8[:, 8:16], in_=imp2[:])
                    ns = nsel[tt % 2]
                    P.V("tensor_scalar", imp.k + m8.k, ns.k, out=ns[:], in0=imp[:], scalar1=m8[:, 15:16], scalar2=NEG,
                        op0=ALU.is_lt, op1=ALU.mult)
                    nx = nsx[tt % 2]
                    nkeys = (tt + 1) * 128
                    P.V("tensor_copy", ns.k, nx.k, out=nx[:, 0:nkeys].rearrange("t (b s) -> t b s", s=64),
                        in_=ns[:, 0:2 * (tt + 1)].unsqueeze(2).to_broadcast([128, 2 * (tt + 1), 64]))

                def attn_stage(tt):
                    nonlocal it
                    s_ = sm[tt % 2]
                    tq = slice(tt * 128, (tt + 1) * 128)
                    nx = nsx[tt % 2]
                    Z, ri, cf, oc = s_["Z"], s_["ri"], s_["cf"], s_["oc"]
                    kb0 = max(0, tt - 4)
                    its = [("s", kb) for kb in range(tt + 1)] + [("w", kb) for kb in range(kb0, tt + 1)]
                    base = it
                    it += len(its)

                    def st1(n):
                        br, kb = its[n]
                        ps = pss[(base + n) % NPS]; e_ = eb[(base + n) % NPS]
                        ks_ = slice(kb * 128, (kb + 1) * 128)
                        if br == "s":
                            P.mm(ksa.k + Qa.k, ps.k, ps[:, :], ksa[:, ks_], Qa[:, :, tq], start=True, stop=False)
                            P.mm(nx.k + ident4.k, ps.k, ps[:, :], nx[:, ks_], ident4[:, :, :], start=False, stop=(kb != tt))
                            if kb == tt:
                                P.mm(negC.k + ident4.k, ps.k, ps[:, :], negC[:], ident4[:, :, :], start=False, stop=True)
                        else:
                            lowm = (kb == tt - 4)
                            P.mm(kwa.k + Qa.k, ps.k, ps[:, :], kwa[:, ks_], Qa[:, :, tq], start=True,
                                 stop=not (lowm or kb == tt))
                            if kb == tt:
                                P.mm(negC.k + ident4.k, ps.k, ps[:, :], negC[:], ident4[:, :, :], start=False, stop=True)
                            if lowm:
                                P.mm(negL.k + ident4.k, ps.k, ps[:, :], negL[:], ident4[:, :, :], start=False, stop=True)
                        P.A(ps.k, e_.k, e_[:, :, :], ps[:, :].rearrange("c (r t) -> c r t", r=4), AF.Exp)

                    def st2(n):
                        br, kb = its[n]
                        e_ = eb[(base + n) % NPS]
                        for r in range(4):
                            if br == "s":
                                P.mm(e_.k + vsa.k, psos.k, psos[:, r, :], e_[:, r, :], vsa[:, kb, 0:65],
                                     start=(kb == 0 and r == 0), stop=(kb == tt and r == 3), sgc=True)
                            else:
                                P.mm(e_.k + vwa.k, psow.k, psow[:, r, :], e_[:, r, :], vwa[:, kb, 0:65],
                                     start=(kb == kb0 and r == 0), stop=(kb == tt and r == 3), sgc=True)

                    LK = NPS - 1
                    for n in range(min(LK, len(its))):
                        st1(n)
                    for n in range(len(its)):
                        if n + LK < len(its):
                            st1(n + LK)
                        st2(n)
                    P.V("tensor_copy", psos.k, Z.k, out=Z[:, 4:8], in_=psos[:, :, 64])
                    P.V("tensor_copy", psow.k, Z.k, out=Z[:, 8:12], in_=psow[:, :, 64])
                    P.V("reciprocal", Z.k, ri.k, out=ri[:, 4:12], in_=Z[:, 4:12])
                    gv = gates[:, tt, 12 * g:12 * g + 12].rearrange("t (r b) -> t b r", b=3)
                    P.V("tensor_tensor", ri.k + gates.k, cf.k, out=cf[:, :].rearrange("t (b r) -> t b r", r=4),
                        in0=ri[:, :].rearrange("t (b r) -> t b r", r=4), in1=gv, op=ALU.mult)
                    t1, t2 = s_["t1"], s_["t2"]
                    bc = lambda b: cf[:, 4 * b:4 * b + 4].unsqueeze(2).to_broadcast([128, 4, 64])
                    P.V("tensor_tensor", oc.k + cf.k, t1.k, out=t1[:], in0=oc[:], in1=bc(0), op=ALU.mult)
                    P.V("tensor_tensor", psos.k + cf.k, t2.k, out=t2[:], in0=psos[:, :, 0:64], in1=bc(1), op=ALU.mult)
                    P.G("tensor_tensor", t1.k + t2.k, t1.k, out=t1[:], in0=t1[:], in1=t2[:], op=ALU.add)
                    P.V("tensor_tensor", psow.k + cf.k, t2.k, out=t2[:], in0=psow[:, :, 0:64], in1=bc(2), op=ALU.mult)
                    yk = ytok[tt % 2]
                    P.G("tensor_tensor", t1.k + t2.k, yk.k, out=yk[:, :].rearrange("t (r d) -> t r d", r=4), in0=t1[:],
                        in1=t2[:], op=ALU.add)
                    for hf in range(2):
                        P.tr(yk.k + identb.k, pst.k, pst[:, hf, :], yk[:, hf * 128:(hf + 1) * 128], identb[:])
                    ys = yst[(tt // 4) % 2]
                    evac(pst.k, ys.k, ys[:, :, (tt % 4) * 128:(tt % 4 + 1) * 128], pst[:, :, :])
                    if tt % 4 == 3:
                        ch = tt // 4
                        P.dma(SP, ys.k, [], ynsaT_d[:, 2 * g:2 * g + 2, ch * 512:(ch + 1) * 512], ys[:])

                sel_stage(0)
                for tt in range(NT):
                    if tt + 1 < NT:
                        sel_stage(tt + 1)
                    attn_stage(tt)
                P.barrier()
              P.es = es
            P.barrier()
        P.es = top

    def layer_norm(pre, gt, bt, st6, mv, out):
        for hf in range(2):
            P.V("bn_stats", pre.k, st6.k, out=st6[:, hf * 6:(hf + 1) * 6], in_=pre[:, hf * 512:(hf + 1) * 512])
        P.V("bn_aggr", st6.k, mv.k, out=mv[:, 0:2], in_=st6[:, :])
        P.V("tensor_scalar", mv.k, mv.k, out=mv[:, 2:3], in0=mv[:, 1:2], scalar1=EPS, scalar2=None, op0=ALU.add)
        P.A(mv.k, mv.k, mv[:, 2:3], mv[:, 2:3], AF.Sqrt)
        P.V("reciprocal", mv.k, mv.k, out=mv[:, 3:4], in_=mv[:, 2:3])
        P.V("tensor_scalar", pre.k + mv.k, pre.k, out=pre[:], in0=pre[:], scalar1=mv[:, 0:1], scalar2=mv[:, 3:4],
            op0=ALU.subtract, op1=ALU.mult)
        P.V("tensor_tensor", pre.k + gt.k, pre.k, out=pre[:], in0=pre[:], in1=gt[:], op=ALU.mult)
        P.V("tensor_tensor", pre.k + bt.k, out.k, out=out[:], in0=pre[:], in1=bt[:], op=ALU.add)

    def phase_merge(l, res_tok):
        with contextlib.ExitStack() as es:
            P.es = es
            xcs = [Buf(P, "xc%d" % i, [128, 8, 512], BF16) for i in range(2)]
            Wg = Buf(P, "WgC", [128, 8, 2048], BF16)
            Wbr = Buf(P, "Wbr", [128, 8, D], BF16)
            Wo = Buf(P, "Wo", [128, 8, D], BF16)
            gt = Buf(P, "g1", [128, D], F32)
            bt = Buf(P, "b1", [128, D], F32)
            P.dma(POOL, [], Wg.k, Wg[:, :, 0:1024], win_cols(l, O_GSB, 1024))
            P.dma(POOL, [], Wg.k, Wg[:, :, 1024:2048], win_cols(l, O_GNSA, 1024))
            P.dma(POOL, [], Wbr.k, Wbr[:, 0:4, :], w_br_sb[l].rearrange("(kc k) n -> k kc n", k=128))
            P.dma(POOL, [], Wbr.k, Wbr[:, 4:8, :], w_br_nsa[l].rearrange("(kc k) n -> k kc n", k=128))
            P.dma(POOL, [], Wo.k, Wo[:], w_o[l].rearrange("(kc k) n -> k kc n", k=128))
            P.dma(SP, [], gt.k, gt[:], ln1_g[l:l + 1, :].to_broadcast([128, D]))
            P.dma(SP, [], bt.k, bt[:], ln1_b[l:l + 1, :].to_broadcast([128, D]))
            yin = [Buf(P, "yin%d" % i, [128, 8, 512], BF16) for i in range(2)]
            mg = [Buf(P, "mg%d" % i, [128, 8, 512], BF16) for i in range(2)]
            sgs = [Buf(P, "sgs%d" % i, [128, 512], F32) for i in range(2)]
            sgn = [Buf(P, "sgn%d" % i, [128, 512], F32) for i in range(2)]
            ht = [Buf(P, "ht%d" % i, [128, D], F32) for i in range(2)]
            pre = [Buf(P, "pre%d" % i, [128, D], F32) for i in range(2)]
            h1 = [Buf(P, "h1_%d" % i, [128, D], F32) for i in range(2)]
            h1b = [Buf(P, "h1b%d" % i, [128, D], BF16) for i in range(2)]
            st6 = [Buf(P, "st6_%d" % i, [128, 12], F32) for i in range(2)]
            mv = [Buf(P, "mv%d" % i, [128, 4], F32) for i in range(2)]
            stg = [Buf(P, "stg%d" % i, [128, 8, 512], BF16) for i in range(2)]
            ps4 = [Buf(P, "psC%d" % i, [128, 512], F32, psum=True) for i in range(4)]
            pso = [Buf(P, "psCo%d" % i, [128, 2, 512], F32, psum=True) for i in range(1)]
            pst = [Buf(P, "psCt%d" % i, [128, 8, 128], BF16, psum=True) for i in range(2)]
            it = 0
            for ch in range(8):
                tsl = slice(ch * 512, (ch + 1) * 512)
                yi, m_ = yin[ch % 2], mg[ch % 2]
                xc = xcs[ch % 2]
                P.dma(SP, [], xc.k, xc[:], hT_d[:, :, tsl])
                P.dma(SP, [], yi.k, yi[:, 0:4, :], ysbT_d[:, :, tsl])
                P.dma(SP, [], yi.k, yi[:, 4:8, :], ynsaT_d[:, :, tsl])
                for fc in range(8):
                    i2 = it % 2; it += 1
                    pa, pb, pc, pd = ps4
                    fs = slice(fc * 128, (fc + 1) * 128)
                    for kc in range(8):
                        P.mm(Wg.k + xc.k, pa.k, pa[:, :], Wg[:, kc, fs], xc[:, kc, :], start=(kc == 0), stop=(kc == 7))
                    for kc in range(8):
                        P.mm(Wg.k + xc.k, pb.k, pb[:, :], Wg[:, kc, 1024 + fc * 128:1024 + (fc + 1) * 128], xc[:, kc, :],
                             start=(kc == 0), stop=(kc == 7))
                    for kc in range(4):
                        P.mm(Wbr.k + yi.k, pc.k, pc[:, :], Wbr[:, kc, fs], yi[:, kc, :], start=(kc == 0), stop=(kc == 3))
                    for kc in range(4):
                        P.mm(Wbr.k + yi.k, pd.k, pd[:, :], Wbr[:, 4 + kc, fs], yi[:, 4 + kc, :], start=(kc == 0), stop=(kc == 3))
                    a_, b_ = sgs[i2], sgn[i2]
                    P.A(pa.k, a_.k, a_[:], pa[:, :], AF.Sigmoid)
                    P.A(pb.k, b_.k, b_[:], pb[:, :], AF.Sigmoid)
                    P.V("tensor_tensor", a_.k + pc.k, a_.k, out=a_[:], in0=a_[:], in1=pc[:, :], op=ALU.mult)
                    P.V("tensor_tensor", b_.k + pd.k, b_.k, out=b_[:], in0=b_[:], in1=pd[:, :], op=ALU.mult)
                    P.G("tensor_tensor", a_.k + b_.k, m_.k, out=m_[:, fc, :], in0=a_[:], in1=b_[:], op=ALU.add)
                sg_ = stg[ch % 2]
                for t4 in range(4):
                    tt = ch * 4 + t4
                    i2 = tt % 2
                    h_, pr, h1_, hb = ht[i2], pre[i2], h1[i2], h1b[i2]
                    P.dma(SP, [], h_.k, h_[:], res_tok[tt * 128:(tt + 1) * 128, :])
                    po = pso[0]
                    for hf in range(2):
                        for kc in range(8):
                            P.mm(m_.k + Wo.k, po.k, po[:, hf, :], m_[:, kc, t4 * 128:(t4 + 1) * 128],
                                 Wo[:, kc, hf * 512:(hf + 1) * 512], start=(kc == 0), stop=(kc == 7))
                    P.V("scalar_tensor_tensor", h_.k + po.k, pr.k, out=pr[:, :].rearrange("t (a b) -> t a b", a=2),
                        in0=h_[:, :].rearrange("t (a b) -> t a b", a=2), scalar=ALPHA, in1=po[:, :, :],
                        op0=ALU.mult, op1=ALU.add)
                    layer_norm(pr, gt, bt, st6[i2], mv[i2], h1_)
                    P.dma(SP, h1_.k, [], h1_tok[tt * 128:(tt + 1) * 128, :], h1_[:])
                    P.A(h1_.k, hb.k, hb[:], h1_[:], AF.Copy)
                    pt = pst[i2]
                    for kc in range(8):
                        P.tr(hb.k + identb.k, pt.k, pt[:, kc, :], hb[:, kc * 128:(kc + 1) * 128], identb[:])
                    evac(pt.k, sg_.k, sg_[:, :, t4 * 128:(t4 + 1) * 128], pt[:, :, :])
                P.dma(SP, sg_.k, [], h1T_d[:, :, tsl], sg_[:])
            P.barrier()
        P.es = top

    def phase_moe(l, out_tok, outT_d):
        with contextlib.ExitStack() as es:
            P.es = es
            wr = Buf(P, "wr", [128, 8, NE], F32)
            brt = Buf(P, "brt", [128, NE], F32)
            bd = Buf(P, "bd", [NE, D], F32)
            bgu = Buf(P, "bgu", [128, NE, 16], F32)
            bgs = Buf(P, "bgs", [128, NE, 8], F32)
            bgu1 = Buf(P, "bgu1", [128, NE, 8], F32)
            P.dma(SP, [], wr.k, wr[:], w_router[l].rearrange("(kc k) n -> k kc n", k=128))
            P.dma(SP, [], brt.k, brt[:], b_router[l:l + 1, :].to_broadcast([128, NE]))
            P.dma(SP, [], bd.k, bd[:], b_down[l])
            P.dma(SP, [], bgu.k, bgu[:], b_guT[l])
            P.V("tensor_scalar", bgu.k, bgs.k, out=bgs[:], in0=bgu[:, :, 0:8], scalar1=1.702, scalar2=None, op0=ALU.mult)
            P.V("tensor_scalar", bgu.k, bgu1.k, out=bgu1[:], in0=bgu[:, :, 8:16], scalar1=1.0, scalar2=None, op0=ALU.add)
            hq = Buf(P, "hq", [128, 8, 1024], BF16)
            acc = Buf(P, "acc", [128, 8, D], F32, n=8)
            comb = Buf(P, "comb", [128, 8, NE], F32, n=8)
            for qd in range(4):
                t0 = qd * 1024
                P.dma(SP, [], hq.k, hq[:], h1T_d[:, :, t0:t0 + 1024])
                with contextlib.ExitStack() as es1:
                    P.es = es1
                    wpg = Buf(P, "wpg", [128, 8, D], BF16)
                    wpp = Buf(P, "wpp", [128, 2, D], BF16)
                    P.dma(POOL, [], wpg.k, wpg[:], w_ple_gate[l].rearrange("(kc k) n -> k kc n", k=128))
                    P.dma(POOL, [], wpp.k, wpp[:], w_ple_proj[l].rearrange("(kc k) n -> k kc n", k=128))
                    h1t = [Buf(P, "h1t%d" % i, [128, D], F32) for i in range(2)]
                    hTf = [Buf(P, "hTf%d" % i, [128, 8, 128], F32) for i in range(2)]
                    pb = [Buf(P, "pb%d" % i, [128, 256], BF16) for i in range(2)]
                    pT = [Buf(P, "pT%d" % i, [128, 2, 128], BF16) for i in range(2)]
                    lg = [Buf(P, "lg%d" % i, [128, NE], F32) for i in range(2)]
                    ex = [Buf(P, "ex%d" % i, [128, NE], F32) for i in range(2)]
                    m8 = [Buf(P, "m8r%d" % i, [128, 16], F32) for i in range(2)]
                    cT = [Buf(P, "cT%d" % i, [NE, 128], F32) for i in range(2)]
                    sgp = [Buf(P, "sgp%d" % i, [128, D], F32) for i in range(2)]
                    pstf = [Buf(P, "pstf%d" % i, [128, 4, 128], F32, psum=True) for i in range(2)]
                    pstb = Buf(P, "pstb", [128, 8, 128], BF16, psum=True)
                    psl = Buf(P, "psl", [128, 512], F32, psum=True)
                    psb = Buf(P, "psb", [128, 2, 512], F32, psum=True)
                    pspg = Buf(P, "pspg", [128, 2, 512], F32, psum=True)
                    for t8 in range(8):
                        tt = qd * 8 + t8
                        i2 = t8 % 2
                        hh, hf_, lg_, ex_, m_ = h1t[i2], hTf[i2], lg[i2], ex[i2], m8[i2]
                        tl = slice(t8 * 128, (t8 + 1) * 128)
                        P.dma(SP, [], hh.k, hh[:], h1_tok[tt * 128:(tt + 1) * 128, :])
                        for half in range(2):
                            pt = pstf[half]
                            for c4 in range(4):
                                kc = half * 4 + c4
                                P.tr(hh.k + identf.k, pt.k, pt[:, c4, :], hh[:, kc * 128:(kc + 1) * 128], identf[:])
                            evac(pt.k, hf_.k, hf_[:, half * 4:half * 4 + 4, :], pt[:, :, :])
                        for kc in range(8):
                            P.mm(hf_.k + wr.k, psl.k, psl[:, 0:NE], hf_[:, kc, :], wr[:, kc, :], start=(kc == 0), stop=(kc == 7))
                        P.V("tensor_tensor", psl.k + brt.k, lg_.k, out=lg_[:], in0=psl[:, 0:NE], in1=brt[:], op=ALU.add)
                        P.V("max", lg_.k, m_.k, out=m_[:, 0:8], in_=lg_[:])
                        P.V("tensor_scalar", m_.k, m_.k, out=m_[:, 8:9], in0=m_[:, 0:1], scalar1=-1.0, scalar2=None, op0=ALU.mult)
                        P.A(lg_.k + m_.k, ex_.k, ex_[:], lg_[:], AF.Exp, bias=m_[:, 8:9])
                        P.V("scalar_tensor_tensor", lg_.k + m_.k + ex_.k, ex_.k, out=ex_[:], in0=lg_[:], scalar=m_[:, 3:4],
                            in1=ex_[:], op0=ALU.is_ge, op1=ALU.mult)
                        P.V("tensor_reduce", ex_.k, m_.k, out=m_[:, 9:10], in_=ex_[:], axis=AX.X, op=ALU.add)
                        P.V("reciprocal", m_.k, m_.k, out=m_[:, 10:11], in_=m_[:, 9:10])
                        P.V("tensor_scalar", ex_.k + m_.k, [comb.k[t8]], out=comb[:, t8, :], in0=ex_[:], scalar1=m_[:, 10:11],
                            scalar2=None, op0=ALU.mult)
                        pt = pstf[0]
                        P.tr([comb.k[t8]] + identf.k, pt.k, pt[0:NE, 0, :], comb[:, t8, :], identf[:])
                        c_ = cT[i2]
                        evac(pt.k, c_.k, c_[:], pt[0:NE, 0, :])
                        for hf in range(2):
                            P.mm(c_.k + bd.k, psb.k, psb[:, hf, :], c_[:], bd[:, hf * 512:(hf + 1) * 512])
                        pb_, pT_ = pb[i2], pT[i2]
                        P.dma(POOL, [], pb_.k, pb_[:], p[l, tt * 128:(tt + 1) * 128, :])
                        for c2 in range(2):
                            P.tr(pb_.k + identb.k, pstb.k, pstb[:, c2, :], pb_[:, c2 * 128:(c2 + 1) * 128], identb[:])
                        evac(pstb.k, pT_.k, pT_[:, :, :], pstb[:, 0:2, :])
                        for hf in range(2):
                            for kc in range(8):
                                P.mm(hq.k + wpg.k, pspg.k, pspg[:, hf, :], hq[:, kc, tl], wpg[:, kc, hf * 512:(hf + 1) * 512],
                                     start=(kc == 0), stop=(kc == 7))
                        sp_ = sgp[i2]
                        P.A(pspg.k, sp_.k, sp_[:, :].rearrange("t (a b) -> t a b", a=2), pspg[:, :, :], AF.Sigmoid)
                        for hf in range(2):
                            for kc in range(2):
                                P.mm(pT_.k + wpp.k, pspg.k, pspg[:, hf, :], pT_[:, kc, :], wpp[:, kc, hf * 512:(hf + 1) * 512],
                                     start=(kc == 0), stop=(kc == 1))
                        P.V("tensor_tensor", sp_.k + pspg.k, sp_.k, out=sp_[:, :].rearrange("t (a b) -> t a b", a=2),
                            in0=sp_[:, :].rearrange("t (a b) -> t a b", a=2), in1=pspg[:, :, :], op=ALU.mult)
                        ak = [acc.k[t8]]
                        av = acc[:, t8, :].rearrange("t (a b) -> t a b", a=2)
                        P.V("scalar_tensor_tensor", hh.k + psb.k, ak, out=av, in0=hh[:, :].rearrange("t (a b) -> t a b", a=2),
                            scalar=ALPHA, in1=psb[:, :, :], op0=ALU.mult, op1=ALU.add)
                        P.G("tensor_tensor", ak + sp_.k, ak, out=acc[:, t8, :], in0=acc[:, t8, :], in1=sp_[:], op=ALU.add)
                    P.barrier()
                P.es = es
                es2 = contextlib.ExitStack()
                P.es = es2
                wgu = [Buf(P, "wgu%d" % i, [128, 8, 2 * DFF], BF16) for i in range(2)]
                wdn = [Buf(P, "wdn%d" % i, [128, 8, D], BF16) for i in range(2)]
                actT = [Buf(P, "actT%d" % i, [128, 8, 512], BF16) for i in range(2)]
                sgb = [Buf(P, "sgb%d" % i, [128, 512], BF16) for i in range(2)]
                gmb = [Buf(P, "gmb%d" % i, [128, 512], BF16) for i in range(2)]
                umb = [Buf(P, "umb%d" % i, [128, 512], BF16) for i in range(2)]
                psg = [Buf(P, "psg%d" % i, [128, 512], F32, psum=True) for i in range(2)]
                psu = [Buf(P, "psu%d" % i, [128, 512], F32, psum=True) for i in range(2)]
                psd = [Buf(P, "psd%d" % i, [128, 2, 512], F32, psum=True) for i in range(2)]

                wst = [Buf(P, "wst%d" % i, [128, 1024], F32) for i in range(3)]
                NPC = 24

                def piece(e, n):
                    kc, pc = divmod(n, 3)
                    if pc < 2:
                        return w_gu[l, e, kc * 128:(kc + 1) * 128, pc * 1024:(pc + 1) * 1024]
                    return w_down[l, e, kc * 128:(kc + 1) * 128, :]

                def expert_loader(e, slot):
                    for n in range(min(2, NPC)):
                        P.dma(SP, [], wst[n % 3].k, wst[n % 3][:], piece(e, n))
                    for n in range(NPC):
                        if n + 2 < NPC:
                            P.dma(SP, [], wst[(n + 2) % 3].k, wst[(n + 2) % 3][:], piece(e, n + 2))
                        kc, pc = divmod(n, 3)
                        sb_ = wst[n % 3]
                        if pc < 2:
                            P.A(sb_.k, wgu[slot].k, wgu[slot][:, kc, pc * 1024:(pc + 1) * 1024], sb_[:], AF.Copy)
                        else:
                            P.A(sb_.k, wdn[slot].k, wdn[slot][:, kc, :], sb_[:], AF.Copy)
                        yield

                for _ in expert_loader(0, 0):
                    pass
                it = 0
                for e in range(NE):
                    slot = e % 2
                    ldr = expert_loader(e + 1, (e + 1) % 2) if e + 1 < NE else iter(())
                    wg_, wd_ = wgu[slot], wdn[slot]
                    for tc in range(2):
                        tsl = slice(tc * 512, (tc + 1) * 512)
                        at = actT[tc % 2]
                        for j in range(8):
                            i2 = it % 2; it += 1
                            pg, pu = psg[i2], psu[i2]
                            next(ldr, None)
                            next(ldr, None)
                            for kc in range(8):
                                P.mm(wg_.k + hq.k, pg.k, pg[:, :], wg_[:, kc, j * 128:(j + 1) * 128], hq[:, kc, tsl],
                                     start=(kc == 0), stop=(kc == 7))
                            for kc in range(8):
                                P.mm(wg_.k + hq.k, pu.k, pu[:, :], wg_[:, kc, DFF + j * 128:DFF + (j + 1) * 128], hq[:, kc, tsl],
                                     start=(kc == 0), stop=(kc == 7))
                            s_, g_, u_ = sgb[i2], gmb[i2], umb[i2]
                            P.A(pg.k + bgs.k, s_.k, s_[:], pg[:, :], AF.Sigmoid, scale=1.702, bias=bgs[:, e, j:j + 1])
                            P.V("tensor_scalar", pg.k + bgu.k, g_.k, out=g_[:], in0=pg[:, :], scalar1=bgu[:, e, j:j + 1],
                                scalar2=7.0, op0=ALU.add, op1=ALU.min)
                            P.V("tensor_scalar", pu.k + bgu1.k, u_.k, out=u_[:], in0=pu[:, :], scalar1=bgu1[:, e, j:j + 1],
                                scalar2=8.0, op0=ALU.add, op1=ALU.min)
                            P.V("scalar_tensor_tensor", s_.k + g_.k, g_.k, out=g_[:], in0=s_[:], scalar=SIG_MAX, in1=g_[:],
                                op0=ALU.min, op1=ALU.mult)
                            P.V("scalar_tensor_tensor", g_.k + u_.k, at.k, out=at[:, j, :], in0=u_[:], scalar=-6.0, in1=g_[:],
                                op0=ALU.max, op1=ALU.mult)
                        for t4 in range(4):
                            t8 = tc * 4 + t4
                            pd = psd[t4 % 2]
                            for hf in range(2):
                                for j in range(8):
                                    P.mm(at.k + wd_.k, pd.k, pd[:, hf, :], at[:, j, t4 * 128:(t4 + 1) * 128],
                                         wd_[:, j, hf * 512:(hf + 1) * 512], start=(j == 0), stop=(j == 7))
                            ak = [acc.k[t8]]
                            av = acc[:, t8, :].rearrange("t (a b) -> t a b", a=2)
                            P.V("scalar_tensor_tensor", pd.k + ak + [comb.k[t8]], ak, out=av, in0=pd[:, :, :],
                                scalar=comb[:, t8, e:e + 1], in1=av, op0=ALU.mult, op1=ALU.add)
                    for _ in ldr:
                        pass
                P.barrier()
                es2.close()
                P.es = es
                with contextlib.ExitStack() as es3:
                    P.es = es3
                    st6 = [Buf(P, "st6D%d" % i, [128, 12], F32) for i in range(2)]
                    mv = [Buf(P, "mvD%d" % i, [128, 4], F32) for i in range(2)]
                    h2 = [Buf(P, "h2_%d" % i, [128, D], F32) for i in range(2)]
                    h2b = [Buf(P, "h2b%d" % i, [128, D], BF16) for i in range(2)]
                    stg = [Buf(P, "stgD%d" % i, [128, 8, 512], BF16) for i in range(2)]
                    pst = [Buf(P, "psDt%d" % i, [128, 8, 128], BF16, psum=True) for i in range(2)]
                    gt = Buf(P, "g2", [128, D], F32)
                    bt = Buf(P, "b2", [128, D], F32)
                    P.dma(SP, [], gt.k, gt[:], ln2_g[l:l + 1, :].to_broadcast([128, D]))
                    P.dma(SP, [], bt.k, bt[:], ln2_b[l:l + 1, :].to_broadcast([128, D]))
                    for t8 in range(8):
                        tt = qd * 8 + t8
                        i2 = t8 % 2
                        a2 = acc[:, t8, :]
                        ak = [acc.k[t8]]
                        s6, m_ = st6[i2], mv[i2]
                        for hf in range(2):
                            P.V("bn_stats", ak, s6.k, out=s6[:, hf * 6:(hf + 1) * 6], in_=acc[:, t8, hf * 512:(hf + 1) * 512])
                        P.V("bn_aggr", s6.k, m_.k, out=m_[:, 0:2], in_=s6[:, :])
                        P.V("tensor_scalar", m_.k, m_.k, out=m_[:, 2:3], in0=m_[:, 1:2], scalar1=EPS, scalar2=None, op0=ALU.add)
                        P.A(m_.k, m_.k, m_[:, 2:3], m_[:, 2:3], AF.Sqrt)
                        P.V("reciprocal", m_.k, m_.k, out=m_[:, 3:4], in_=m_[:, 2:3])
                        P.V("tensor_scalar", ak + m_.k, ak, out=a2, in0=a2, scalar1=m_[:, 0:1], scalar2=m_[:, 3:4],
                            op0=ALU.subtract, op1=ALU.mult)
                        P.V("tensor_tensor", ak + gt.k, ak, out=a2, in0=a2, in1=gt[:], op=ALU.mult)
                        h2_ = h2[i2]
                        P.V("tensor_tensor", ak + bt.k, h2_.k, out=h2_[:], in0=a2, in1=bt[:], op=ALU.add)
                        P.dma(SP, h2_.k, [], out_tok[tt * 128:(tt + 1) * 128, :], h2_[:])
                        if outT_d is not None:
                            hb = h2b[i2]
                            P.A(h2_.k, hb.k, hb[:], h2_[:], AF.Copy)
                            pt = pst[i2]
                            for kc in range(8):
                                P.tr(hb.k + identb.k, pt.k, pt[:, kc, :], hb[:, kc * 128:(kc + 1) * 128], identb[:])
                            sg_ = stg[(t8 // 4) % 2]
                            evac(pt.k, sg_.k, sg_[:, :, (t8 % 4) * 128:(t8 % 4 + 1) * 128], pt[:, :, :])
                            if t8 % 4 == 3:
                                c0 = qd * 1024 + (t8 // 4) * 512
                                P.dma(SP, sg_.k, [], outT_d[:, :, c0:c0 + 512], sg_[:])
                    P.barrier()
                P.es = es
            P.barrier()
        P.es = top

    with contextlib.ExitStack() as es0:
        P.es = es0
        tok_to_T(es0, x, hT_d, "p0")
        P.barrier()
    P.es = top
    for l in range(n_layers):
        res_tok = x if l == 0 else h_tok
        last = (l == n_layers - 1)
        with contextlib.ExitStack() as esl:
            P.es = esl
            XT = Buf(P, "XT", [128, 8, S], BF16)
            P.dma(SP, [], XT.k, XT[:, 0:4, :], hT_d[:, 0:4, :])
            P.dma(SP, [], XT.k, XT[:, 4:8, :], hT_d[:, 4:8, :])
            if "A" in phases:
                phase_sb(l, XT)
            if "B" in phases:
                phase_nsa(l, XT)
                P.dead = False
            P.barrier()
        P.es = top
        if "C" in phases:
            phase_merge(l, res_tok)
        if "D" in phases:
            phase_moe(l, y if last else h_tok, None if last else hT_d)
    P.barrier()
    top.close()
    return nc, P


def _bf(a):
    return np.asarray(a, dtype=np.float32).astype(ml_dtypes.bfloat16)


def make_consts():
    i = np.arange(128)
    c = {}
    c["c_identb"] = _bf(np.eye(128))
    c["c_identf"] = np.eye(128, dtype=np.float32)
    c["c_triui"] = _bf(i[:, None] >= i[None, :])
    c["c_ones"] = _bf(np.ones((128, 128)))
    ii = np.arange(512)
    c["c_masksb"] = _bf(ii[None, None, :] > (128 * np.arange(4)[None, :, None] + i[:, None, None]))
    c["c_ident4"] = _bf(np.repeat(np.eye(128)[:, None, :], 4, axis=1))
    c["c_negC"] = _bf(np.where(i[None, :] > i[:, None], NEG, 0.0))
    c["c_negL"] = _bf(np.where(i[None, :] <= i[:, None], NEG, 0.0))
    tt = np.arange(NT)
    c["c_maskcmp"] = _bf((128 * tt[None, :, None] + i[None, None, :]) >= (32 * i[:, None, None] + 31))
    c["c_pairsum"] = _bf((i[:, None] // 2) == np.arange(64)[None, :])
    pos = 128 * tt[None, :, None] + i[:, None, None]
    cur = pos // 64
    b = np.arange(64)[None, None, :]
    forced = (b == 0) | (b == cur) | (b == cur - 1)
    valid = (b * 64) <= pos
    c["c_F"] = _bf(np.where(forced, 1.0e6, 0.0))
    c["c_V"] = _bf(np.where(valid, 0.0, -1.0e30))
    slopes = 2.0 ** (-8.0 * np.arange(1, 9) / 8.0)
    t = np.arange(S)
    qa = np.zeros((4, 8, S), np.float32)
    for h in range(8):
        qa[0, h, :] = 128.0 * slopes[h]
        qa[1, h, :] = slopes[h]
        qa[2, h, :] = -128.0 * slopes[h] * (t // 128)
        qa[3, h, :] = -slopes[h] * (t % 128)
    c["c_qaug"] = _bf(qa)
    ka = np.stack([t // 128, t % 128, np.ones(S), np.ones(S)]).astype(np.float32)
    c["c_kaug"] = _bf(ka)
    ce = 32 * np.arange(128) + 31
    c["c_kcaug"] = _bf(np.stack([ce // 128, ce % 128, np.ones(128), np.ones(128)]).astype(np.float32))
    return c


_CACHE = {}


def make_in_maps(x, p, w_in, w_cmp1, w_cmp2, pe_cmp, w_br_sb, w_br_nsa, w_o, ln1_g, ln1_b, ln2_g, ln2_b,
                 w_router, b_router, w_gu, b_gu, w_down, b_down, w_ple_gate, w_ple_proj, cores=range(8)):
    f = lambda a: np.ascontiguousarray(np.asarray(a, dtype=np.float32))
    shared = dict(
        w_in=f(w_in), w_cmp1=f(w_cmp1), w_cmp2=f(w_cmp2),
        pe_cmpT=f(np.transpose(np.asarray(pe_cmp), (0, 1, 3, 2))),
        w_br_sb=f(w_br_sb), w_br_nsa=f(w_br_nsa), w_o=f(w_o), ln1_g=f(ln1_g), ln1_b=f(ln1_b),
        ln2_g=f(ln2_g), ln2_b=f(ln2_b), w_router=f(w_router), b_router=f(b_router), w_gu=f(w_gu),
        b_guT=f(np.transpose(np.asarray(b_gu).reshape(DEPTH, NE, 16, 128), (0, 3, 1, 2))),
        w_down=f(w_down), b_down=f(b_down), w_ple_gate=f(w_ple_gate), w_ple_proj=f(w_ple_proj),
    )
    shared.update(make_consts())
    x = np.asarray(x, dtype=np.float32)
    p = np.asarray(p, dtype=np.float32)
    in_maps = []
    for b in cores:
        m = dict(shared)
        m["x"] = np.ascontiguousarray(x[b])
        m["p"] = np.ascontiguousarray(p[:, b])
        in_maps.append(m)
    return in_maps


def kernel(**inputs):
    if "nc" not in _CACHE:
        _CACHE["nc"] = build()[0]
    nc = _CACHE["nc"]
    in_maps = make_in_maps(**inputs)
    res = run_bass_kernel_spmd(nc, in_maps, core_ids=list(range(8)))
    return np.stack([np.asarray(r["y"], dtype=np.float32) for r in res.results], axis=0)
```
